# Optimizing a Trainium2 kernel written in Bass

```python
import jax, jax.numpy as jnp
from jax import lax
import numpy as np


D_MODEL = 2048
BATCH = 4
SEQ = 8192
DEPTH = 1
DEC_BATCH = 8
DEC_SEQ = 64
PAST_LEN = 1024

CHUNK = 64
WINDOW = 128
WIN_CHUNKS = WINDOW // CHUNK
CONV_WIDTH = 3
HEAD_DIM = 64
N_Q_HEADS = (D_MODEL // 2) // HEAD_DIM
N_KV_HEADS = 4
GQA_GROUP = N_Q_HEADS // N_KV_HEADS
ATTN_WIDTH = N_Q_HEADS * HEAD_DIM
KV_WIDTH = N_KV_HEADS * HEAD_DIM
D_CONV = D_MODEL // 2
MIX_WIDTH = D_CONV + ATTN_WIDTH
N_IN = 3 * D_CONV + ATTN_WIDTH + 2 * KV_WIDTH
ROT_DIM = HEAD_DIM // 4
ROPE_THETA = 500000.0
N_MEM = 256
N_MEM_HEADS = 4
MEM_HEAD_DIM = D_MODEL // N_MEM_HEADS
D_FF = 11 * D_MODEL // 4
EPS = 1e-6
NEG_INF = -1e30

kernel_name = 'hybrid_streaming_encoder_step'


def rmsnorm(x, g):
    xf = x.astype(jnp.float32)
    y = xf * lax.rsqrt(jnp.mean(xf * xf, axis=-1, keepdims=True) + EPS) * g.astype(jnp.float32)
    return y.astype(x.dtype)


def rope_partial(x, pos):
    inv = ROPE_THETA ** (-jnp.arange(0, ROT_DIM, 2, dtype=jnp.float32) / ROT_DIM)
    ang = pos[:, None] * inv[None, :]
    cos = jnp.cos(ang)[:, None, :]
    sin = jnp.sin(ang)[:, None, :]
    xr = x[..., :ROT_DIM].astype(jnp.float32)
    x1, x2 = xr[..., :ROT_DIM // 2], xr[..., ROT_DIM // 2:]
    rot = jnp.concatenate([x1 * cos - x2 * sin, x2 * cos + x1 * sin], axis=-1).astype(x.dtype)
    return jnp.concatenate([rot, x[..., ROT_DIM:]], axis=-1)


def causal_dwconv(u, prev, w):
    t = u.shape[1]
    up = jnp.concatenate([prev.astype(u.dtype), u], axis=1)
    y = w[0] * up[:, 0:t]
    for i in range(1, CONV_WIDTH):
        y = y + w[i] * up[:, i:i + t]
    return y, up[:, -(CONV_WIDTH - 1):]


def sink_attention(q, k, v, sinks, mask):
    s = jnp.einsum('...qkgd,...lkd->...kgql', q, k).astype(jnp.float32) * (HEAD_DIM ** -0.5)
    if mask is not None:
        s = jnp.where(mask, s, NEG_INF)
    sink = jnp.broadcast_to(sinks.astype(jnp.float32)[:, :, None, None], s.shape[:-1] + (1,))
    p = jax.nn.softmax(jnp.concatenate([s, sink], axis=-1), axis=-1)[..., :-1]
    return jnp.einsum('...kgql,...lkd->...qkgd', p.astype(v.dtype), v)


def swa_prompt(q, k, v, sinks):
    n, t = q.shape[0], q.shape[1]
    nc = t // CHUNK
    qb = q.reshape(n, nc, CHUNK, N_KV_HEADS, GQA_GROUP, HEAD_DIM)

    def band(a):
        ab = a.reshape(n, nc, CHUNK, N_KV_HEADS, HEAD_DIM)
        ap = jnp.pad(ab, ((0, 0), (WIN_CHUNKS, 0), (0, 0), (0, 0), (0, 0)))
        return jnp.concatenate([ap[:, j:j + nc] for j in range(WIN_CHUNKS + 1)], axis=2)

    band_len = (WIN_CHUNKS + 1) * CHUNK
    key_chunk = jnp.arange(nc)[:, None] - WIN_CHUNKS + jnp.arange(band_len)[None, :] // CHUNK
    mask = (key_chunk >= 0)[:, None, None, None, :]
    o = sink_attention(qb, band(k), band(v), sinks.reshape(N_KV_HEADS, GQA_GROUP), mask)
    keep = min(WINDOW, t)
    return o.reshape(n, t, ATTN_WIDTH), k[:, -keep:], v[:, -keep:]


def swa_sample(q, k, v, sinks, cache_k, cache_v):
    n, t = q.shape[0], q.shape[1]
    kk = jnp.concatenate([cache_k.astype(k.dtype), k], axis=1)
    vv = jnp.concatenate([cache_v.astype(v.dtype), v], axis=1)
    qg = q.reshape(n, t, N_KV_HEADS, GQA_GROUP, HEAD_DIM)
    o = sink_attention(qg, kk, vv, sinks.reshape(N_KV_HEADS, GQA_GROUP), None)
    keep = cache_k.shape[1]
    return o.reshape(n, t, ATTN_WIDTH), kk[:, -keep:], vv[:, -keep:]


def memory_kv(mem, g_mem, w_xk, w_xv):
    n = mem.shape[0]
    hm = rmsnorm(mem, g_mem)
    mk = (hm @ w_xk).reshape(n, N_MEM, N_MEM_HEADS, MEM_HEAD_DIM)
    mv = (hm @ w_xv).reshape(n, N_MEM, N_MEM_HEADS, MEM_HEAD_DIM)
    return mk, mv


def trunk_layer(x, pos, mem_k, mem_v, mix_prev, ffn_prev, attend,
                g_mix_pre, w_mix_in, conv_mix_w, g_grp_conv, g_grp_attn, attn_sinks,
                w_mix_out, g_mix_post, g_x_pre, w_xq, w_xo, g_x_post,
                g_ffn_pre, w_gate, w_up, conv_ffn_w, conv_ffn_b, w_down, g_ffn_post):
    n, t, _ = x.shape
    h = rmsnorm(x, g_mix_pre)
    z = h @ w_mix_in
    cuts = [D_CONV, 2 * D_CONV, 3 * D_CONV, 3 * D_CONV + ATTN_WIDTH, 3 * D_CONV + ATTN_WIDTH + KV_WIDTH]
    b_gate, c_gate, u, q, k, v = jnp.split(z, cuts, axis=-1)
    conv_out, mix_state = causal_dwconv(c_gate * u, mix_prev, conv_mix_w)
    y_conv = b_gate * conv_out
    q = rope_partial(q.reshape(n, t, N_Q_HEADS, HEAD_DIM), pos)
    k = rope_partial(k.reshape(n, t, N_KV_HEADS, HEAD_DIM), pos)
    v = v.reshape(n, t, N_KV_HEADS, HEAD_DIM)
    y_attn, swa_k, swa_v = attend(q, k, v, attn_sinks)
    y = jnp.concatenate([rmsnorm(y_conv, g_grp_conv), rmsnorm(y_attn, g_grp_attn)], axis=-1) @ w_mix_out
    x = x + rmsnorm(y, g_mix_post)
    hq = (rmsnorm(x, g_x_pre) @ w_xq).reshape(n, t, N_MEM_HEADS, MEM_HEAD_DIM)
    s = jnp.einsum('nthd,nmhd->nhtm', hq, mem_k).astype(jnp.float32) * (MEM_HEAD_DIM ** -0.5)
    pr = jax.nn.softmax(s, axis=-1).astype(mem_v.dtype)
    o = jnp.einsum('nhtm,nmhd->nthd', pr, mem_v).reshape(n, t, N_MEM_HEADS * MEM_HEAD_DIM)
    x = x + rmsnorm(o @ w_xo, g_x_post)
    h = rmsnorm(x, g_ffn_pre)
    a, ffn_state = causal_dwconv(h @ w_gate, ffn_prev, conv_ffn_w)
    y = (jax.nn.silu(a + conv_ffn_b) * (h @ w_up)) @ w_down
    x = x + rmsnorm(y, g_ffn_post)
    return x, swa_k, swa_v, mix_state, ffn_state


def setup_inputs(seed: int = 0) -> dict:
    key = jax.random.key(seed)
    ks = jax.random.split(key, 32)
    f32 = jnp.float32

    def nrm(k, shape, scale):
        return jax.random.normal(k, shape, f32) * scale

    def gain(k, width):
        return 1.0 + 0.01 * jax.random.normal(k, (DEPTH, width), f32)

    swa_len = min(WINDOW, PAST_LEN)
    return {
        'x_prompt': nrm(ks[0], (BATCH, SEQ, D_MODEL), 1.0),
        'x_sample': nrm(ks[1], (DEC_BATCH, DEC_SEQ, D_MODEL), 1.0),
        'cache_mem_k': nrm(ks[2], (DEPTH, DEC_BATCH, N_MEM, N_MEM_HEADS, MEM_HEAD_DIM), 1.0),
        'cache_mem_v': nrm(ks[3], (DEPTH, DEC_BATCH, N_MEM, N_MEM_HEADS, MEM_HEAD_DIM), 1.0),
        'cache_swa_k': nrm(ks[4], (DEPTH, DEC_BATCH, swa_len, N_KV_HEADS, HEAD_DIM), 1.0),
        'cache_swa_v': nrm(ks[5], (DEPTH, DEC_BATCH, swa_len, N_KV_HEADS, HEAD_DIM), 1.0),
        'state_mix_conv': nrm(ks[6], (DEPTH, DEC_BATCH, CONV_WIDTH - 1, D_CONV), 1.0),
        'state_ffn_conv': nrm(ks[7], (DEPTH, DEC_BATCH, CONV_WIDTH - 1, D_FF), 1.0),
        'mem_prompt': nrm(ks[8], (BATCH, N_MEM, D_MODEL), 1.0),
        'g_mix_pre': gain(ks[9], D_MODEL),
        'w_mix_in': nrm(ks[10], (DEPTH, D_MODEL, N_IN), D_MODEL ** -0.5),
        'conv_mix_w': nrm(ks[11], (DEPTH, CONV_WIDTH, D_CONV), CONV_WIDTH ** -0.5),
        'g_grp_conv': gain(ks[12], D_CONV),
        'g_grp_attn': gain(ks[13], ATTN_WIDTH),
        'attn_sinks': nrm(ks[14], (DEPTH, N_Q_HEADS), 1.0),
        'w_mix_out': nrm(ks[15], (DEPTH, MIX_WIDTH, D_MODEL), MIX_WIDTH ** -0.5),
        'g_mix_post': gain(ks[16], D_MODEL),
        'g_mem': gain(ks[17], D_MODEL),
        'w_xk': nrm(ks[18], (DEPTH, D_MODEL, N_MEM_HEADS * MEM_HEAD_DIM), D_MODEL ** -0.5),
        'w_xv': nrm(ks[19], (DEPTH, D_MODEL, N_MEM_HEADS * MEM_HEAD_DIM), D_MODEL ** -0.5),
        'g_x_pre': gain(ks[20], D_MODEL),
        'w_xq': nrm(ks[21], (DEPTH, D_MODEL, N_MEM_HEADS * MEM_HEAD_DIM), D_MODEL ** -0.5),
        'w_xo': nrm(ks[22], (DEPTH, N_MEM_HEADS * MEM_HEAD_DIM, D_MODEL), (N_MEM_HEADS * MEM_HEAD_DIM) ** -0.5),
        'g_x_post': gain(ks[23], D_MODEL),
        'g_ffn_pre': gain(ks[24], D_MODEL),
        'w_gate': nrm(ks[25], (DEPTH, D_MODEL, D_FF), D_MODEL ** -0.5),
        'w_up': nrm(ks[26], (DEPTH, D_MODEL, D_FF), D_MODEL ** -0.5),
        'conv_ffn_w': nrm(ks[27], (DEPTH, CONV_WIDTH, D_FF), CONV_WIDTH ** -0.5),
        'conv_ffn_b': nrm(ks[28], (DEPTH, D_FF), 0.01),
        'w_down': nrm(ks[29], (DEPTH, D_FF, D_MODEL), D_FF ** -0.5),
        'g_ffn_post': gain(ks[30], D_MODEL),
    }


def reference(x_prompt, x_sample, cache_mem_k, cache_mem_v, cache_swa_k, cache_swa_v,
              state_mix_conv, state_ffn_conv, mem_prompt,
              g_mix_pre, w_mix_in, conv_mix_w, g_grp_conv, g_grp_attn, attn_sinks,
              w_mix_out, g_mix_post, g_mem, w_xk, w_xv, g_x_pre, w_xq, w_xo, g_x_post,
              g_ffn_pre, w_gate, w_up, conv_ffn_w, conv_ffn_b, w_down, g_ffn_post):
    n_p, s_p, _ = x_prompt.shape
    n_s, s_s, _ = x_sample.shape
    pos_p = jnp.arange(s_p, dtype=jnp.float32)
    pos_s = PAST_LEN + jnp.arange(s_s, dtype=jnp.float32)
    zero_mix = jnp.zeros((n_p, CONV_WIDTH - 1, D_CONV), x_prompt.dtype)
    zero_ffn = jnp.zeros((n_p, CONV_WIDTH - 1, D_FF), x_prompt.dtype)

    yp, ys = x_prompt, x_sample
    mk_p, mv_p, sk_p, sv_p, mc_p, fc_p = [], [], [], [], [], []
    sk_s, sv_s, mc_s, fc_s = [], [], [], []
    for l in range(DEPTH):
        w = (g_mix_pre[l], w_mix_in[l], conv_mix_w[l], g_grp_conv[l], g_grp_attn[l], attn_sinks[l],
             w_mix_out[l], g_mix_post[l], g_x_pre[l], w_xq[l], w_xo[l], g_x_post[l],
             g_ffn_pre[l], w_gate[l], w_up[l], conv_ffn_w[l], conv_ffn_b[l], w_down[l], g_ffn_post[l])
        mk, mv = memory_kv(mem_prompt, g_mem[l], w_xk[l], w_xv[l])
        yp, a_k, a_v, a_mc, a_fc = trunk_layer(yp, pos_p, mk, mv, zero_mix, zero_ffn, swa_prompt, *w)
        mk_p.append(mk)
        mv_p.append(mv)
        sk_p.append(a_k)
        sv_p.append(a_v)
        mc_p.append(a_mc)
        fc_p.append(a_fc)
        attend_s = (lambda q, k, v, s, ck=cache_swa_k[l], cv=cache_swa_v[l]:
                    swa_sample(q, k, v, s, ck, cv))
        ys, b_k, b_v, b_mc, b_fc = trunk_layer(ys, pos_s, cache_mem_k[l], cache_mem_v[l],
                                               state_mix_conv[l], state_ffn_conv[l], attend_s, *w)
        sk_s.append(b_k)
        sv_s.append(b_v)
        mc_s.append(b_mc)
        fc_s.append(b_fc)

    return (yp, ys,
            jnp.stack(mk_p), jnp.stack(mv_p), jnp.stack(sk_p), jnp.stack(sv_p),
            jnp.stack(mc_p), jnp.stack(fc_p),
            jnp.stack(sk_s), jnp.stack(sv_s), jnp.stack(mc_s), jnp.stack(fc_s))
```

```python
import os
import numpy as np
from contextlib import ExitStack
import concourse.bass as bass
import concourse.mybir as mybir
from concourse.bass_utils import run_bass_kernel_spmd

F32 = mybir.dt.float32
BF16 = mybir.dt.bfloat16
AF = mybir.ActivationFunctionType
ALU = mybir.AluOpType
AX = mybir.AxisListType

D = 2048
NIN = 4608
DFF = 5632
FC = 44
EPS = 1e-6
NSLOT = 3
SEG = 4096
HALO = 256
TS = 64
TOT = TS + HALO + SEG
N_CORES = 8

WSHAPES = {
    "mix_in": (D, NIN), "mix_out": (D, D), "xk": (D, D), "xv": (D, D), "xq": (D, D),
    "xo": (D, D), "gate": (D, DFF), "up": (D, DFF), "down": (DFF, D),
}
WORDER = ["mix_in", "mix_out", "xq", "xo", "gate", "up", "down", "xk", "xv"]

PPO = {}
_o = 0
for _n, _w in [("g_mix_pre", 16), ("g_x_pre", 16), ("g_ffn_pre", 16), ("g_mem", 16), ("g_conv", 8),
               ("g_attn", 8), ("cw", 24), ("fw", 132), ("fb", 44), ("sinks", 16), ("flag", 1),
               ("eps", 1), ("zero", 1), ("mask", 20)]:
    PPO[_n] = _o
    _o += _w
NPP = _o


class Buf:
    __slots__ = ("w", "r")

    def __init__(self):
        self.w = None
        self.r = {}


class DSem:
    def __init__(self, h):
        self.h = h
        self.total = 0


class Eng:
    def __init__(self, name, h, sem):
        self.name, self.h, self.sem = name, h, sem
        self.n = 0
        self.waited = {}


def tile_plan(halo=False):
    u = []
    for r in range(4):
        for part in (1, 2, 0):
            u.append(("mix_in", 0, 16, part * 1024 + r * 256, 256))
    for r in range(4):
        u.append(("mix_in", 0, 16, 3072 + r * 256, 256))
    u.append(("mix_in", 0, 16, 4096, 256))
    u.append(("mix_in", 0, 16, 4352, 256))
    for ob in range(4):
        for un in range(2):
            u.append(("mix_out", un * 8, 8, ob * 512, 512))
    for r in range(8):
        u.append(("xq", 0, 16, r * 256, 256))
    for ob in range(4):
        for un in range(2):
            u.append(("xo", un * 8, 8, ob * 512, 512))
    for p in range(22):
        u.append(("gate", 0, 16, p * 256, 256))
        if not halo:
            u.append(("up", 0, 16, p * 256, 256))
    if not halo:
        for ob in range(4):
            for un in range(6):
                u.append(("down", un * 8, min(8, FC - un * 8), ob * 512, 512))
    return u


def memkv_plan():
    u = []
    for r in range(8):
        u.append(("xk", 0, 16, r * 256, 256))
    for ob in range(4):
        for un in range(2):
            u.append(("xv", un * 8, 8, ob * 512, 512))
    return u


class StopBuild(Exception):
    pass


class Builder:
    def __init__(self, nt_main=8):
        self.nt_main = nt_main
        self.stop = os.environ.get("KSTOP", "")
        self.ntile = 0
        self.nc = bass.Bass("TRN2", target_bir_lowering=False)

    def _waits(self, eng, reads, writes):
        need = {}

        def add(tok):
            if tok is None:
                return
            s, v = tok
            if isinstance(s, DSem):
                v = s.total
                key, h = id(s), s.h
            else:
                key, h = id(s), s
            if key not in need or need[key][1] < v:
                need[key] = (h, v)

        for b in reads:
            add(b.w)
        for b in writes:
            add(b.w)
            for t in b.r.values():
                add(t)
        for key, (h, v) in need.items():
            if eng.name == "pe" and h is eng.sem:
                continue
            if eng.waited.get(key, 0) < v:
                eng.h.wait_ge(h, v)
                eng.waited[key] = v

    def do(self, eng, fn, reads=(), writes=(), reg_only=()):
        self._waits(eng, reads, writes)
        ins = fn()
        eng.n += 1
        ins.then_inc(eng.sem, 1)
        tok = (eng.sem, eng.n)
        for b in writes:
            b.w = tok
            b.r = {}
        for b in reads:
            b.r[id(eng.sem)] = tok
        for b in reg_only:
            b.r[id(eng.sem)] = tok

    def dma(self, q, out, in_, dsem, reads=(), writes=(), nowait=False):
        if not nowait:
            self._waits(q, reads, writes)
        q.h.dma_start(out=out, in_=in_).then_inc(dsem.h, 16)
        dsem.total += 16
        tok = (dsem, dsem.total)
        for b in writes:
            b.w = tok
            b.r = {}
        for b in reads:
            b.r[id(dsem)] = tok

    def ckpt(self, label):
        if self.stop and self.stop == f"{self.ntile}:{label}":
            raise StopBuild()

    def ring(self):
        i = self.bank_i % 8
        self.bank_i += 1
        return self.banks[i], self.bankb[i]

    def wview(self, s, nk, ncols):
        return self.WS[:, s, 0:nk * ncols].rearrange("p (k n) -> p k n", k=nk)

    def w_issue(self, i):
        name, k0, nk, c0, ncols = self.plan[i]
        s = i % NSLOT
        src = self.wbf[name][k0 * 128:(k0 + nk) * 128, c0:c0 + ncols].rearrange("(k p) n -> p k n", p=128)
        self.dma(self.SP, self.wview(s, nk, ncols), src, self.wsem[s], reads=[self.wmatb[name]], writes=[self.Wb[s]])

    def w_next(self, spec):
        assert self.plan[self.widx] == spec, (self.widx, self.plan[self.widx], spec)
        while self.wissued < min(len(self.plan), self.widx + NSLOT):
            self.w_issue(self.wissued)
            self.wissued += 1
        s = self.widx % NSLOT
        self.widx += 1
        return self.wview(s, spec[2], spec[4]), self.Wb[s]

    def pp(self, name, i=0, n=1, rows=slice(0, 128)):
        o = PPO[name] + i
        return self.PP[rows, o:o + n]

    def proj_fm(self, W, Wb, j, T, nk=16, split=False):
        bank, bb = self.ring()
        nc = self.nc
        A = self.A
        if split and T == 512:
            for tb in range(4):
                def f():
                    ins = None
                    for kc in range(nk):
                        ins = nc.tensor.matmul(bank[:, tb * 128:(tb + 1) * 128], lhsT=W[:, kc, j * 128:(j + 1) * 128], rhs=A[:, kc, tb * 128:(tb + 1) * 128],
                                               start=(kc == 0), stop=(kc == nk - 1))
                    return ins

                self.do(self.PE, f, reads=[Wb, self.Atb[tb]], writes=[bb], reg_only=self.Ab[0:nk])
            return bank, bb

        def f():
            ins = None
            for kc in range(nk):
                ins = nc.tensor.matmul(bank[:, 0:T], lhsT=W[:, kc, j * 128:(j + 1) * 128], rhs=A[:, kc, 0:T],
                                       start=(kc == 0), stop=(kc == nk - 1))
            return ins

        self.do(self.PE, f, reads=[Wb] + self.Ab[0:nk], writes=[bb])
        return bank, bb

    def prenorm(self, T, TB, nblk, gname):
        nc = self.nc
        X, XN, SM, A = self.X, self.XN, self.SM, self.A
        ss4, sd4, rs4 = SM[:TB, 0:nblk], SM[:TB, 4:4 + nblk], SM[:TB, 36:36 + nblk]
        ssb, sdb, rsb = self.SMb[0], self.SMb[4], self.SMb[36]
        self.do(self.E2, lambda: self.E2.h.memset(ss4, 0.0), writes=[ssb])
        for tb in range(nblk):
            self.do(self.ACT, lambda: nc.scalar.activation(out=XN[:TB, tb % 2, :], in_=X[:TB, tb, :], func=AF.Square, accum_out=SM[:TB, tb:tb + 1]),
                    reads=[self.Xb[tb], ssb], writes=[self.XNb[tb % 2], ssb])
        self.do(self.ACT, lambda: nc.scalar.activation(out=sd4, in_=ss4, func=AF.Sqrt, bias=self.pp("eps", rows=slice(0, TB)), scale=1.0 / D),
                reads=[ssb], writes=[sdb])
        self.do(self.DVE, lambda: nc.vector.reciprocal(out=rs4, in_=sd4), reads=[sdb], writes=[rsb])
        for tb in range(nblk):
            i = tb % 2
            rs = SM[:TB, 36 + tb:37 + tb]
            if tb % 2 == 0:
                self.do(self.ACT, lambda: nc.scalar.activation(out=XN[:TB, i, :], in_=X[:TB, tb, :], func=AF.Copy, scale=rs),
                        reads=[self.Xb[tb], rsb], writes=[self.XNb[i]])
            else:
                self.do(self.DVE, lambda: nc.vector.tensor_scalar(out=XN[:TB, i, :], in0=X[:TB, tb, :], scalar1=rs, scalar2=None, op0=ALU.mult),
                        reads=[self.Xb[tb], rsb], writes=[self.XNb[i]])
            for q4 in range(4):
                bank, bb = self.ring()
                pbb = bank[:].bitcast(BF16)

                def f():
                    ins = None
                    for a in range(4):
                        kc = q4 * 4 + a
                        ins = nc.tensor.transpose(pbb[:, a * TB:(a + 1) * TB], XN[:TB, i, kc * 128:(kc + 1) * 128], self.IDB[:TB, :TB])
                    return ins

                self.do(self.PE, f, reads=[self.XNb[i]], writes=[bb])
                gap = self.pp(gname, q4 * 4, 4).unsqueeze(2).to_broadcast([128, 4, TB])
                self.do(self.DVE, lambda: nc.vector.tensor_tensor(out=A[:, q4 * 4:(q4 + 1) * 4, tb * TB:(tb + 1) * TB],
                                                                  in0=pbb[:, 0:4 * TB].rearrange("p (a b) -> p a b", a=4), in1=gap, op=ALU.mult),
                        reads=[bb], writes=self.Ab[q4 * 4:(q4 + 1) * 4] + [self.Atb[tb]])

    def group_norm(self, T, base, sqbase, gname, slot):
        nc = self.nc
        AR = self.AR
        bank, bb = self.ring()

        def f():
            ins = None
            for i in range(8):
                ins = nc.tensor.matmul(bank[:, 0:T], lhsT=self.ONESB[:, :], rhs=AR[:, sqbase + i, 0:T], start=(i == 0), stop=(i == 7))
            return ins

        self.do(self.PE, f, reads=self.ARb[sqbase:sqbase + 8], writes=[bb])
        rsb_ap = self.RSB[:, 0, 0:T]
        self.do(self.ACT, lambda: nc.scalar.activation(out=rsb_ap, in_=bank[:, 0:T], func=AF.Sqrt, bias=self.pp("eps"), scale=1.0 / 1024),
                reads=[bb], writes=[self.RSBb[slot]])
        self.do(self.DVE, lambda: nc.vector.reciprocal(out=rsb_ap, in_=rsb_ap), reads=[self.RSBb[slot]], writes=[self.RSBb[slot]])
        for i in range(8):
            self.do(self.DVE, lambda: nc.vector.scalar_tensor_tensor(out=AR[:, base + i, 0:T], in0=AR[:, base + i, 0:T], scalar=self.pp(gname, i),
                                                                     in1=rsb_ap, op0=ALU.mult, op1=ALU.mult),
                    reads=[self.ARb[base + i], self.RSBb[slot]], writes=[self.ARb[base + i]])

    def out_proj(self, name, nkc, src, srcb, ys, ysb, gi, T, TB, nblk):
        nc = self.nc
        SM, X = self.SM, self.X
        nun = (nkc + 7) // 8
        yss = SM[:, 8:24]
        self.do(self.E2, lambda: self.E2.h.memset(yss, 0.0), writes=[self.SMb[8]])
        for ob in range(4):
            banks = [self.ring() for _ in range(nblk)]
            for un in range(nun):
                k0 = un * 8
                nk = min(8, nkc - k0)
                W, Wb = self.w_next((name, k0, nk, ob * 512, 512))
                for tb in range(nblk):
                    bank, bb = banks[tb]

                    def f():
                        ins = None
                        for kk in range(nk):
                            ins = nc.tensor.matmul(bank[:TB, :], lhsT=src[:, k0 + kk, tb * TB:(tb + 1) * TB], rhs=W[:, kk, :],
                                                   start=(k0 + kk == 0), stop=(k0 + kk == nkc - 1))
                        return ins

                    self.do(self.PE, f, reads=[Wb] + srcb[k0:k0 + nk], writes=[bb])
            for tb in range(nblk):
                bank, bb = banks[tb]
                self.do(self.ACT, lambda: nc.scalar.activation(out=ys[:TB, tb, ob * 512:(ob + 1) * 512], in_=bank[:TB, :], func=AF.Copy),
                        reads=[bb], writes=[ysb[4 * tb + ob]])
                col = 8 + tb * 4 + ob
                self.do(self.ACT, lambda: nc.scalar.activation(out=self.XN[:TB, 0, 0:512], in_=bank[:TB, :], func=AF.Square, accum_out=SM[:TB, col:col + 1]),
                        reads=[bb, self.SMb[8]], writes=[self.XNb[0], self.SMb[8]])
                self.do(self.DVE, lambda: nc.vector.tensor_tensor(out=ys[:TB, tb, ob * 512:(ob + 1) * 512], in0=ys[:TB, tb, ob * 512:(ob + 1) * 512],
                                                                  in1=self.GB[:TB, gi, ob * 512:(ob + 1) * 512], op=ALU.mult),
                        reads=[ysb[4 * tb + ob], self.GBb], writes=[ysb[4 * tb + ob]])
        st4, sd4, rs4 = SM[:TB, 24:24 + nblk], SM[:TB, 28:28 + nblk], SM[:TB, 32:32 + nblk]
        self.do(self.DVE, lambda: nc.vector.tensor_reduce(out=st4, in_=SM[:TB, 8:8 + 4 * nblk].rearrange("p (t o) -> p t o", o=4), axis=AX.X, op=ALU.add),
                reads=[self.SMb[8]], writes=[self.SMb[24]])
        self.do(self.ACT, lambda: nc.scalar.activation(out=sd4, in_=st4, func=AF.Sqrt, bias=self.pp("eps", rows=slice(0, TB)), scale=1.0 / D),
                reads=[self.SMb[24]], writes=[self.SMb[28]])
        self.do(self.DVE, lambda: nc.vector.reciprocal(out=rs4, in_=sd4), reads=[self.SMb[28]], writes=[self.SMb[32]])
        for tb in range(nblk):
            yb = ysb[4 * tb:4 * tb + 4]
            self.do(self.DVE, lambda: nc.vector.scalar_tensor_tensor(out=X[:TB, tb, :], in0=ys[:TB, tb, :], scalar=SM[:TB, 32 + tb:33 + tb], in1=X[:TB, tb, :],
                                                                     op0=ALU.mult, op1=ALU.add),
                    reads=yb + [self.SMb[32], self.Xb[tb]], writes=[self.Xb[tb]])

    def rope(self, src_ap, srcb, sw_bank, swb, T, out_ap, outb):
        nc = self.nc
        T1, CS = self.T1, self.CS
        self.do(self.DVE, lambda: nc.vector.tensor_tensor(out=T1[:, 0, 0:T], in0=src_ap, in1=CS[:, 0, 0:T], op=ALU.mult),
                reads=[srcb, self.CSb], writes=[self.T1b[0]])
        self.do(self.DVE, lambda: nc.vector.tensor_tensor(out=T1[:, 1, 0:T], in0=sw_bank[:, 0:T], in1=CS[:, 1, 0:T], op=ALU.mult),
                reads=[swb, self.CSb], writes=[self.T1b[1]])
        self.do(self.DVE, lambda: nc.vector.tensor_tensor(out=out_ap, in0=T1[:, 0, 0:T], in1=T1[:, 1, 0:T], op=ALU.add),
                reads=[self.T1b[0], self.T1b[1]], writes=[outb])

    def tile(self, T, xsrc, ydst, tcol, tile_idx, last_out=None, halo=False):
        nc = self.nc
        PE, ACT, DVE, POOL = self.PE, self.ACT, self.DVE, self.POOL
        TB = min(128, T)
        nblk = T // TB
        nch = T // 64
        X, A, AR, KD, VD = self.X, self.A, self.AR, self.KD, self.VD
        CB, CV = self.CB, self.CV
        for tb in range(nblk):
            self.dma(self.SP, X[:TB, tb, :], xsrc[tb * TB:(tb + 1) * TB, :], self.xsem[tb], writes=[self.Xb[tb]])
        for k in range(2):
            self.dma(self.SP, self.CS[:, k, 0:T], self.cs_d[k, :, tcol:tcol + T], self.csem_t, writes=[self.CSb])
        self.ckpt("load")
        self.prenorm(T, TB, nblk, "g_mix_pre")
        self.ckpt("prenorm")
        for r in range(4):
            W, Wb = self.w_next(("mix_in", 0, 16, 1024 + r * 256, 256))
            for j in range(2):
                i = 2 * r + j
                bank, bb = self.proj_fm(W, Wb, j, T, split=(r == 0))
                self.do(ACT, lambda: nc.scalar.activation(out=CB[:, j, 2:2 + T], in_=bank[:, 0:T], func=AF.Copy), reads=[bb], writes=[self.CBb[j]])
                self.do(self.E2, lambda: self.E2.h.tensor_copy(out=CB[:, j, 0:2], in_=self.CUH[:, i, :]), reads=[self.CUHb[i]], writes=[self.CBhb[j]])
            W, Wb = self.w_next(("mix_in", 0, 16, 2048 + r * 256, 256))
            for j in range(2):
                i = 2 * r + j
                bank, bb = self.proj_fm(W, Wb, j, T)
                self.do(DVE, lambda: nc.vector.tensor_tensor(out=CB[:, j, 2:2 + T], in0=CB[:, j, 2:2 + T], in1=bank[:, 0:T], op=ALU.mult),
                        reads=[bb, self.CBb[j]], writes=[self.CBb[j]])
                self.do(self.E2, lambda: self.E2.h.tensor_copy(out=self.CUH[:, i, :], in_=CB[:, j, T:T + 2]), reads=[self.CBb[j]], writes=[self.CUHb[i]])
                self.do(DVE, lambda: nc.vector.tensor_scalar(out=CV[:, j, 0:T], in0=CB[:, j, 2:2 + T], scalar1=self.pp("cw", i * 3 + 2), scalar2=None, op0=ALU.mult),
                        reads=[self.CBb[j]], writes=[self.CVb[j]])
                for k in (1, 0):
                    self.do(DVE, lambda: nc.vector.scalar_tensor_tensor(out=CV[:, j, 0:T], in0=CB[:, j, k:k + T], scalar=self.pp("cw", i * 3 + k), in1=CV[:, j, 0:T],
                                                                        op0=ALU.mult, op1=ALU.add),
                            reads=[self.CBb[j], self.CBhb[j], self.CVb[j]], writes=[self.CVb[j]])
            W, Wb = self.w_next(("mix_in", 0, 16, 0 + r * 256, 256))
            for j in range(2):
                i = 2 * r + j
                bank, bb = self.proj_fm(W, Wb, j, T)
                self.do(DVE, lambda: nc.vector.tensor_tensor(out=AR[:, i, 0:T], in0=bank[:, 0:T], in1=CV[:, j, 0:T], op=ALU.mult),
                        reads=[bb, self.CVb[j]], writes=[self.ARb[i]])
                self.do(ACT, lambda: nc.scalar.activation(out=AR[:, 8 + i, 0:T], in_=AR[:, i, 0:T], func=AF.Square), reads=[self.ARb[i]], writes=[self.ARb[8 + i]])
        self.ckpt("conv")
        self.group_norm(T, 0, 8, "g_conv", 0)
        self.ckpt("convnorm")
        def q_tail(i, qb, qbb):
            bank2, bb2 = self.ring()
            self.do(PE, lambda: nc.tensor.matmul(bank2[:, 0:T], lhsT=self.PMB[:, :], rhs=qb, start=True, stop=True), reads=[qbb], writes=[bb2])
            self.rope(qb, qbb, bank2, bb2, T, AR[:, 16 + i, 0:T], self.ARb[16 + i])

        pend_q = None
        for r in range(4):
            W, Wb = self.w_next(("mix_in", 0, 16, 3072 + r * 256, 256))
            for j in range(2):
                i = 2 * r + j
                bank, bb = self.proj_fm(W, Wb, j, T)
                qb, qbb = self.QB[:, i % 2, 0:T], self.QBb[i % 2]
                self.do(ACT, lambda: nc.scalar.activation(out=qb, in_=bank[:, 0:T], func=AF.Copy), reads=[bb], writes=[qbb])
                if pend_q is not None:
                    q_tail(*pend_q)
                pend_q = (i, qb, qbb)
        self.ckpt("q")
        Wk, Wkb = self.w_next(("mix_in", 0, 16, 4096, 256))

        def k_proj(j):
            bank, bb = self.proj_fm(Wk, Wkb, j, T)
            self.do(ACT, lambda: nc.scalar.activation(out=self.KF[:, j, 0:T], in_=bank[:, 0:T], func=AF.Copy), reads=[bb], writes=[self.KFb[j]])

        def k_rope(j):
            self.do(DVE, lambda: nc.vector.tensor_copy(out=self.QB[:, j, 0:T], in_=self.KF[:, j, 0:T]), reads=[self.KFb[j]], writes=[self.QBb[j]])
            bank2, bb2 = self.ring()
            self.do(PE, lambda: nc.tensor.matmul(bank2[:, 0:T], lhsT=self.PMB[:, :], rhs=self.QB[:, j, 0:T], start=True, stop=True), reads=[self.QBb[j]], writes=[bb2])
            self.rope(self.KF[:, j, 0:T], self.KFb[j], bank2, bb2, T, self.KR[:, j, 0:T], self.KRb[j])
            self.do(DVE, lambda: nc.vector.tensor_copy(out=self.KRB[:, j, 0:T], in_=self.KR[:, j, 0:T]), reads=[self.KRb[j]], writes=[self.KRBb[j]])

        def k_sel(j):
            for hh in range(2):
                h = 2 * j + hh
                for ab in range(2):
                    bank3, bb3 = self.ring()
                    self.do(PE, lambda: nc.tensor.matmul(bank3[:, 0:T], lhsT=self.SELB[:, hh * 2 + ab, :], rhs=self.KRB[:, j, 0:T], start=True, stop=True), reads=[self.KRBb[j]], writes=[bb3])
                    self.do(ACT, lambda: nc.scalar.activation(out=KD[:, ab, h, 128:128 + T], in_=bank3[:, 0:T], func=AF.Copy), reads=[bb3], writes=[self.KDtb[h]])

        def v_proj(tb):
            bank, bb = self.ring()

            def f():
                ins = None
                for kc in range(16):
                    ins = nc.tensor.matmul(bank[:TB, 0:256], lhsT=A[:, kc, tb * TB:(tb + 1) * TB], rhs=Wv[:, kc, 0:256], start=(kc == 0), stop=(kc == 15))
                return ins

            self.do(PE, f, reads=[Wvb] + self.Ab, writes=[bb])
            self.do(ACT, lambda: nc.scalar.activation(out=self.VF[:TB, 0, :], in_=bank[:TB, 0:256], func=AF.Copy), reads=[bb], writes=[self.VFb[0]])
            vin = self.VF[:TB, 0, :].rearrange("p (g d) -> p g d", g=4)
            vout = VD[:TB, 1 + tb, :].rearrange("p (g e) -> p g e", e=65)[:, :, 0:64]
            self.do(DVE, lambda: nc.vector.tensor_copy(out=vout, in_=vin), reads=[self.VFb[0]], writes=[self.VDb[1 + tb]])
            if last_out is not None and tb == nblk - 1:
                self.dma(POOL, last_out["v"], self.VF[:TB, 0, :], self.osem, reads=[self.VFb[0]])

        k_proj(0)
        q_tail(*pend_q)
        k_proj(1)
        Wv, Wvb = self.w_next(("mix_in", 0, 16, 4352, 256))
        ksteps = [lambda: k_rope(0), lambda: k_rope(1), lambda: k_sel(0), lambda: k_sel(1)]
        vsteps = [(lambda tb=tb: v_proj(tb)) for tb in range(nblk)]
        for idx in range(max(len(ksteps), len(vsteps))):
            if idx < len(vsteps):
                vsteps[idx]()
            if idx < len(ksteps):
                ksteps[idx]()
        if last_out is not None:
            kdst = last_out["k"]
            for j in range(2):
                self.dma(POOL, kdst[j * 128:(j + 1) * 128, :], self.KR[:, j, T - kdst.shape[1]:T], self.osem, reads=[self.KRb[j]])
        self.ckpt("v")
        units = [(j, g) for j in range(nch) for g in range(4)]
        for k in range(4):
            zr = slice(64, 128) if k < 2 else slice(0, 64)
            self.do(self.E2, lambda: self.E2.h.memset(AR[zr, 24 + k, 256:512], 0.0), writes=[self.ARb[24 + k]])

        def geom(j):
            if j % 2 == 0:
                return j // 2, j // 2 + 1, slice(0, 64), ((j - 2, 0), (j - 1, 0), (j, 1))
            return (j + 1) // 2, (j - 1) // 2, slice(64, 128), ((j - 2, 1), (j - 1, 0), (j, 0))

        def emit_S(u):
            j, g = units[u]
            bank, bb = self.ring()
            mm = geom(j)[3]

            def f():
                ins = None
                for m, reg in mm:
                    par = m % 2
                    for ab in range(2):
                        c0 = reg * 256 + ab * 128
                        ins = nc.tensor.matmul(bank[par * 64:(par + 1) * 64, c0:c0 + 128].rearrange("p (a q) -> p a q", a=2),
                                               lhsT=KD[:, ab, g, (m + 2) * 64:(m + 3) * 64],
                                               rhs=AR[:, 16 + 2 * g:16 + 2 * g + 2, j * 64:(j + 1) * 64], start=True, stop=True)
                return ins

            rd = [self.KDtb[g], self.ARb[16 + 2 * g], self.ARb[17 + 2 * g]]
            if j < 2:
                rd.append(self.KDhb)
            self.do(PE, f, reads=rd, writes=[bb])
            return bank, bb

        def post(u, j, g, od, odb):
            k = u % 2
            rd3 = self.RD[:, 2 * k:2 * k + 2].unsqueeze(2)
            den3 = od[:, 0:130].rearrange("p (a e) -> p a e", e=65)[:, :, 64:65]
            esk3 = self.ESK[:, g * 2:(g + 1) * 2].unsqueeze(2)
            self.do(DVE, lambda: nc.vector.tensor_tensor(out=rd3, in0=den3, in1=esk3, op=ALU.add), reads=[odb, self.ESKb], writes=[self.RDb[k]])
            self.do(DVE, lambda: nc.vector.reciprocal(out=self.RD[:, 2 * k:2 * k + 2], in_=self.RD[:, 2 * k:2 * k + 2]), reads=[self.RDb[k]], writes=[self.RDb[k]])
            for ab in range(2):
                self.do(DVE, lambda: nc.vector.tensor_scalar(out=self.ON[:, k, ab * 64:(ab + 1) * 64], in0=od[:, ab * 65:ab * 65 + 64],
                                                             scalar1=self.RD[:, 2 * k + ab:2 * k + ab + 1], scalar2=None, op0=ALU.mult),
                        reads=[odb, self.RDb[k]], writes=[self.ONb[k]])
            tp, tpb_ = self.ring()
            tpv = tp[:].bitcast(BF16)
            self.do(PE, lambda: nc.tensor.transpose(tpv[:, 0:128], self.ON[:, k, :], self.IDB[:, :]), reads=[self.ONb[k]], writes=[tpb_])
            pend2.append((j, g, tpv, tpb_))

        def post2(j, g, tpv, tpb_):
            self.do(ACT, lambda: nc.scalar.activation(out=AR[:, 8 + 2 * g:8 + 2 * g + 2, j * 64:(j + 1) * 64],
                                                      in_=tpv[:, 0:128].rearrange("p (a q) -> p a q", a=2), func=AF.Copy),
                    reads=[tpb_], writes=[self.ARb[8 + 2 * g], self.ARb[9 + 2 * g]])

        pend = []
        pend2 = []
        cnt = [0, 0]
        Sb = emit_S(0)
        for u, (j, g) in enumerate(units):
            bank, bb = Sb
            pj = j % 2
            pk = 24 + 2 * pj + cnt[pj] % 2
            cnt[pj] += 1
            pt, ptb = AR[:, pk, :], self.ARb[pk]
            blkF, blkH, hrows, _ = geom(j)
            biasF = self.pp("mask", 2 * tile_idx) if j == 0 else self.pp("zero")
            biasH = self.pp("mask", 2 * tile_idx + 1, rows=hrows) if j == 1 else self.pp("zero", rows=hrows)
            self.do(ACT, lambda: nc.scalar.activation(out=pt[:, 0:256], in_=bank[:, 0:256], func=AF.Exp, bias=biasF, scale=0.125), reads=[bb], writes=[ptb])
            self.do(ACT, lambda: nc.scalar.activation(out=pt[hrows, 256:512], in_=bank[hrows, 256:512], func=AF.Exp, bias=biasH, scale=0.125),
                    reads=[bb, ptb], writes=[ptb])
            if u + 1 < len(units):
                Sb = emit_S(u + 1)
            od, odb = self.ring()

            def f():
                ins = None
                for ab in range(2):
                    for idx, (blk, reg) in enumerate(((blkF, 0), (blkH, 1))):
                        c0 = reg * 256 + ab * 128
                        ins = nc.tensor.matmul(od[:, ab * 65:(ab + 1) * 65], lhsT=pt[:, c0:c0 + 128], rhs=VD[:, blk, g * 65:(g + 1) * 65],
                                               start=(idx == 0), stop=(idx == 1))
                return ins

            self.do(PE, f, reads=[ptb, self.VDb[blkF], self.VDb[blkH]], writes=[odb])
            pend.append((u, j, g, od, odb))
            if len(pend2) > 0:
                post2(*pend2.pop(0))
            if len(pend) > 1:
                post(*pend.pop(0))
        while pend:
            post(*pend.pop(0))
        while pend2:
            post2(*pend2.pop(0))
        for i in range(8):
            self.do(ACT, lambda: nc.scalar.activation(out=AR[:, 16 + i, 0:T], in_=AR[:, 8 + i, 0:T], func=AF.Square), reads=[self.ARb[8 + i]], writes=[self.ARb[16 + i]])
        self.group_norm(T, 8, 16, "g_attn", 1)
        self.ckpt("attn")
        if T >= 128:
            self.do(self.E2, lambda: self.E2.h.tensor_copy(out=KD[:, :, :, 0:128].rearrange("p a h k -> p (a h) k"), in_=KD[:, :, :, T:T + 128].rearrange("p a h k -> p (a h) k")), reads=self.KDtb, writes=[self.KDhb])
            self.do(self.E2, lambda: self.E2.h.tensor_copy(out=VD[:, 0, :], in_=VD[:, nblk, :]), reads=[self.VDb[nblk]], writes=[self.VDb[0]])
        ysA = A[:].rearrange("p (t c) f -> p t (c f)", t=4)
        ysB = AR[:, 0:16, :].rearrange("p (t c) f -> p t (c f)", t=4)
        self.out_proj("mix_out", 16, AR, self.ARb, ysA, self.Ab, 0, T, TB, nblk)
        self.ckpt("mixout")
        self.prenorm(T, TB, nblk, "g_x_pre")
        for r in range(8):
            W, Wb = self.w_next(("xq", 0, 16, r * 256, 256))
            for j in range(2):
                c = 2 * r + j
                bank, bb = self.proj_fm(W, Wb, j, T, split=(r == 0))
                self.do(ACT, lambda: nc.scalar.activation(out=AR[:, c, 0:T], in_=bank[:, 0:T], func=AF.Copy), reads=[bb], writes=[self.ARb[c]])
        for h in range(4):
            xo = 28 + 2 * (h % 2)
            xpb = [self.ARb[xo], self.ARb[xo + 1]]
            for mb in range(2):
                bank, bb = self.ring()

                def f():
                    ins = None
                    for kk in range(4):
                        ins = nc.tensor.matmul(bank[:, 0:T], lhsT=self.MK[:, 4 * h + kk, mb * 128:(mb + 1) * 128], rhs=AR[:, 4 * h + kk, 0:T], start=(kk == 0), stop=(kk == 3))
                    return ins

                self.do(PE, f, reads=[self.MKb] + self.ARb[4 * h:4 * h + 4], writes=[bb])
                self.do(ACT, lambda: nc.scalar.activation(out=AR[:, xo + mb, 0:T], in_=bank[:, 0:T], func=AF.Exp, bias=self.pp("zero"), scale=float(512 ** -0.5)), reads=[bb], writes=[xpb[mb]])
            bank, bb = self.ring()

            def f():
                ins = None
                for mb in range(2):
                    ins = nc.tensor.matmul(bank[:, 0:T], lhsT=self.ONESB[:, :], rhs=AR[:, xo + mb, 0:T], start=(mb == 0), stop=(mb == 1))
                return ins

            self.do(PE, f, reads=xpb, writes=[bb])
            rd, rdb = self.T1[:, h % 2, 0:T], self.T1b[h % 2]
            self.do(DVE, lambda: nc.vector.reciprocal(out=rd, in_=bank[:, 0:T]), reads=[bb], writes=[rdb])
            for dc in range(4):
                bank, bb = self.ring()
                fc = 4 * h + dc

                def f():
                    ins = None
                    for mb in range(2):
                        ins = nc.tensor.matmul(bank[:, 0:T], lhsT=self.MV[:, mb, fc * 128:(fc + 1) * 128], rhs=AR[:, xo + mb, 0:T], start=(mb == 0), stop=(mb == 1))
                    return ins

                self.do(PE, f, reads=xpb + [self.MVb], writes=[bb])
                self.do(DVE, lambda: nc.vector.tensor_tensor(out=A[:, fc, 0:T], in0=bank[:, 0:T], in1=rd, op=ALU.mult), reads=[bb, rdb], writes=[self.Ab[fc]])
        self.ckpt("xattn")
        self.out_proj("xo", 16, A, self.Ab, ysB, self.ARb, 1, T, TB, nblk)
        self.ckpt("xo")
        self.prenorm(T, TB, nblk, "g_ffn_pre")
        for p in range(22):
            W, Wb = self.w_next(("gate", 0, 16, p * 256, 256))
            for j in range(2):
                c = 2 * p + j
                bank, bb = self.proj_fm(W, Wb, j, T, split=(p == 0))
                self.do(ACT, lambda: nc.scalar.activation(out=CB[:, j, 2:2 + T], in_=bank[:, 0:T], func=AF.Copy), reads=[bb], writes=[self.CBb[j]])
                self.do(self.E2, lambda: self.E2.h.tensor_copy(out=CB[:, j, 0:2], in_=self.GH[:, c, :]), reads=[self.GHb[c]], writes=[self.CBhb[j]])
                self.do(self.E2, lambda: self.E2.h.tensor_copy(out=self.GH[:, c, :], in_=CB[:, j, T:T + 2]), reads=[self.CBb[j]], writes=[self.GHb[c]])
                if halo:
                    continue
                self.do(DVE, lambda: nc.vector.tensor_scalar(out=CV[:, j, 0:T], in0=CB[:, j, 2:2 + T], scalar1=self.pp("fw", c * 3 + 2), scalar2=self.pp("fb", c), op0=ALU.mult, op1=ALU.add),
                        reads=[self.CBb[j]], writes=[self.CVb[j]])
                for k in (1, 0):
                    self.do(DVE, lambda: nc.vector.scalar_tensor_tensor(out=CV[:, j, 0:T], in0=CB[:, j, k:k + T], scalar=self.pp("fw", c * 3 + k), in1=CV[:, j, 0:T],
                                                                        op0=ALU.mult, op1=ALU.add),
                            reads=[self.CBb[j], self.CBhb[j], self.CVb[j]], writes=[self.CVb[j]])
                if halo:
                    continue
                self.do(ACT, lambda: nc.scalar.activation(out=CV[:, j, 0:T], in_=CV[:, j, 0:T], func=AF.Silu), reads=[self.CVb[j]], writes=[self.CVb[j]])
            if halo:
                continue
            W, Wb = self.w_next(("up", 0, 16, p * 256, 256))
            for j in range(2):
                c = 2 * p + j
                bank, bb = self.proj_fm(W, Wb, j, T)
                self.do(DVE, lambda: nc.vector.tensor_tensor(out=AR[:, c, 0:T], in0=CV[:, j, 0:T], in1=bank[:, 0:T], op=ALU.mult), reads=[self.CVb[j], bb], writes=[self.ARb[c]])
        self.ckpt("ffn")
        if halo:
            self.ntile += 1
            return
        self.out_proj("down", FC, AR, self.ARb, ysA, self.Ab, 2, T, TB, nblk)
        self.ckpt("down")
        if ydst is not None:
            for tb in range(nblk):
                self.dma(POOL, ydst[tb * TB:(tb + 1) * TB, :], X[:TB, tb, :], self.ysem[tb], reads=[self.Xb[tb]])
        self.ckpt("tile")
        self.ntile += 1

    def memkv(self):
        nc = self.nc
        PE, ACT, DVE, POOL = self.PE, self.ACT, self.DVE, self.POOL
        X, A = self.X, self.A
        for tb in range(2):
            self.dma(self.SP, X[:, tb, :], self.memp_d[tb * 128:(tb + 1) * 128, :], self.xsem[tb], writes=[self.Xb[tb]])
        self.prenorm(256, 128, 2, "g_mem")
        for r in range(8):
            W, Wb = self.w_next(("xk", 0, 16, r * 256, 256))
            for j in range(2):
                c = 2 * r + j
                bank, bb = self.proj_fm(W, Wb, j, 256)
                st, stb = self.T1[:, c % 2, 0:256], self.T1b[c % 2]
                self.do(ACT, lambda: nc.scalar.activation(out=st, in_=bank[:, 0:256], func=AF.Copy), reads=[bb], writes=[stb])
                self.do(DVE, lambda: nc.vector.tensor_copy(out=self.MK[:, c, :], in_=st), reads=[stb], writes=[self.MKb])
                self.dma(ACT, self.o_mkT[c * 128:(c + 1) * 128, :], st, self.msem, reads=[stb])
        for ob in range(4):
            banks = [self.ring() for _ in range(2)]
            for un in range(2):
                W, Wb = self.w_next(("xv", un * 8, 8, ob * 512, 512))
                for mb in range(2):
                    bank, bb = banks[mb]

                    def f():
                        ins = None
                        for kk in range(8):
                            ins = nc.tensor.matmul(bank[:, :], lhsT=A[:, un * 8 + kk, mb * 128:(mb + 1) * 128], rhs=W[:, kk, :], start=(un == 0 and kk == 0), stop=(un == 1 and kk == 7))
                        return ins

                    self.do(PE, f, reads=[Wb] + self.Ab[un * 8:un * 8 + 8], writes=[bb])
            for mb in range(2):
                bank, bb = banks[mb]
                st, stb = self.T1[:, mb, :], self.T1b[mb]
                self.do(ACT, lambda: nc.scalar.activation(out=st, in_=bank[:, :], func=AF.Copy), reads=[bb], writes=[stb])
                self.do(DVE, lambda: nc.vector.tensor_copy(out=self.MV[:, mb, ob * 512:(ob + 1) * 512], in_=st), reads=[stb], writes=[self.MVb])
                self.dma(ACT, self.o_mv[mb * 128:(mb + 1) * 128, ob * 512:(ob + 1) * 512], st, self.msem, reads=[stb])

    def build(self):
        nc = self.nc
        dt = nc.dram_tensor
        self.xseg_d = dt("xseg", [HALO + SEG, D], F32, kind="ExternalInput").ap()
        self.xs_d = dt("xs", [TS, D], F32, kind="ExternalInput").ap()
        self.memp_d = dt("memp", [256, D], F32, kind="ExternalInput").ap()
        cmkT_d = dt("cmkT", [D, 256], F32, kind="ExternalInput").ap()
        cmv_d = dt("cmv", [256, D], F32, kind="ExternalInput").ap()
        ckd_d = dt("ckd", [128, 2, 4, 128], F32, kind="ExternalInput").ap()
        cvd_d = dt("cvd", [128, 260], F32, kind="ExternalInput").ap()
        csk_d = dt("csk", [128, 256], F32, kind="ExternalInput").ap()
        csv_d = dt("csv", [128, 256], F32, kind="ExternalInput").ap()
        smix_d = dt("smix", [128, 16], F32, kind="ExternalInput").ap()
        sffn_d = dt("sffn", [128, 88], F32, kind="ExternalInput").ap()
        pp_d = dt("pp", [128, NPP], F32, kind="ExternalInput").ap()
        gpost_d = dt("gpost", [3, D], F32, kind="ExternalInput").ap()
        cmat_d = dt("cmat", [6, 128, 128], F32, kind="ExternalInput").ap()
        self.cs_d = dt("cs", [2, 128, TOT], F32, kind="ExternalInput").ap()
        wsrc = {n: dt("w_" + n, list(WSHAPES[n]), F32, kind="ExternalInput").ap() for n in WORDER}
        self.wbf = {n: dt("wb_" + n, list(WSHAPES[n]), BF16, kind="Internal").ap() for n in WORDER}
        o_y = dt("o_y", [SEG, D], F32, kind="ExternalOutput").ap()
        o_ys = dt("o_ys", [TS, D], F32, kind="ExternalOutput").ap()
        self.o_mkT = dt("o_mkT", [D, 256], F32, kind="ExternalOutput").ap()
        self.o_mv = dt("o_mv", [256, D], F32, kind="ExternalOutput").ap()
        o_kTp = dt("o_kTp", [256, 128], F32, kind="ExternalOutput").ap()
        o_vp = dt("o_vp", [128, 256], F32, kind="ExternalOutput").ap()
        o_mcp = dt("o_mcp", [128, 16], F32, kind="ExternalOutput").ap()
        o_fcp = dt("o_fcp", [128, 88], F32, kind="ExternalOutput").ap()
        o_kTs = dt("o_kTs", [256, 64], F32, kind="ExternalOutput").ap()
        o_vs = dt("o_vs", [64, 256], F32, kind="ExternalOutput").ap()
        o_kso = dt("o_kso", [64, 256], F32, kind="ExternalOutput").ap()
        o_vso = dt("o_vso", [64, 256], F32, kind="ExternalOutput").ap()
        o_mcs = dt("o_mcs", [128, 16], F32, kind="ExternalOutput").ap()
        o_fcs = dt("o_fcs", [128, 88], F32, kind="ExternalOutput").ap()

        with ExitStack() as es:
            ec = es.enter_context
            sb = lambda n, s, d: ec(nc.sbuf_tensor(n, s, d))
            self.X = sb("X", [128, 4, D], F32)
            self.A = sb("A", [128, 16, 512], BF16)
            self.AR = sb("AR", [128, FC, 512], BF16)
            self.KD = sb("KD", [128, 2, 4, 640], BF16)
            self.VD = sb("VD", [128, 5, 260], BF16)
            self.ON = sb("ON", [128, 2, 128], BF16)
            self.RD = sb("RD", [128, 4], F32)
            self.MK = sb("MK", [128, 16, 256], BF16)
            self.MV = sb("MV", [128, 2, D], BF16)
            self.WS = sb("WS", [128, NSLOT, 4096], BF16)
            self.GB = sb("GB", [128, 3, D], BF16)
            self.CB = sb("CB", [128, 2, 514], F32)
            self.CV = sb("CV", [128, 2, 512], F32)
            self.CS = sb("CS", [128, 2, 512], F32)
            self.T1 = sb("T1", [128, 2, 512], F32)
            self.KF = sb("KF", [128, 2, 512], F32)
            self.KR = sb("KR", [128, 2, 512], F32)
            self.ESKB = sb("ESKB", [128, 4, 256], BF16)
            self.INVB = sb("INVB", [128, 128], BF16)
            self.OSB = sb("OSB", [128, 2, 256], BF16)
            self.KRB = sb("KRB", [128, 2, 512], BF16)
            self.QB = sb("QB", [128, 2, 512], BF16)
            self.XN = sb("XN", [128, 2, D], BF16)
            self.VF = sb("VF", [128, 1, 256], F32)
            self.RSB = sb("RSB", [128, 1, 512], F32)
            self.DE = sb("DE", [128, 2, 256], F32)
            self.SM = sb("SM", [128, 40], F32)
            self.CUH = sb("CUH", [128, 8, 2], F32)
            self.GH = sb("GH", [128, FC, 2], F32)
            self.ESK = sb("ESK", [128, 16], F32)
            self.PP = sb("PP", [128, NPP], F32)
            self.IDB = sb("IDB", [128, 128], BF16)
            self.PMB = sb("PMB", [128, 128], BF16)
            self.SELB = sb("SELB", [128, 4, 128], BF16)
            self.ONESB = sb("ONESB", [128, 128], BF16)
            self.banks = [ec(nc.psum_tensor(f"pb{i}", [128, 512], F32)) for i in range(8)]
            self.bankb = [Buf() for _ in range(8)]
            self.bank_i = 0
            sem = lambda n: ec(nc.semaphore(n))
            self.PE = Eng("pe", nc.tensor, sem("s_pe"))
            self.ACT = Eng("act", nc.scalar, sem("s_act"))
            self.DVE = Eng("dve", nc.vector, sem("s_dve"))
            self.POOL = Eng("pool", nc.gpsimd, sem("s_pool"))
            self.SP = Eng("sp", nc.sync, sem("s_sp"))
            self.wsem = [DSem(sem(f"wsem{i}")) for i in range(NSLOT)]
            self.xsem = [DSem(sem(f"xsem{i}")) for i in range(4)]
            self.ysem = [DSem(sem(f"ysem{i}")) for i in range(4)]
            self.csem_t = DSem(sem("csem_t"))
            self.osem = DSem(sem("osem"))
            self.isem = DSem(sem("isem"))
            self.msem = DSem(sem("msem"))
            self.psem = DSem(sem("psem"))
            cvsem = {n: DSem(sem("cv_" + n)) for n in WORDER}
            nb = lambda n: [Buf() for _ in range(n)]
            self.Xb, self.Ab, self.ARb = nb(4), nb(16), nb(FC)
            self.Atb = nb(4)
            self.KDtb, self.KDhb, self.VDb = nb(4), Buf(), nb(5)
            self.MKb, self.MVb, self.Wb, self.GBb = Buf(), Buf(), nb(NSLOT), Buf()
            self.CBb, self.CBhb, self.CVb, self.CSb, self.T1b = nb(2), nb(2), nb(2), Buf(), nb(2)
            self.KFb, self.KRb, self.KRBb, self.QBb, self.OSBb = nb(2), nb(2), nb(2), nb(2), nb(2)
            self.ONb, self.RDb = nb(2), nb(2)
            self.XNb, self.VFb, self.DEb = nb(2), nb(2), nb(2)
            _r = Buf()
            self.RSBb = [_r, _r]
            self.SMb = nb(40)
            for i in range(9, 24):
                self.SMb[i] = self.SMb[8]
            for base in (0, 4, 24, 28, 32, 36):
                for i in range(base + 1, base + 4):
                    self.SMb[i] = self.SMb[base]
            self.CUHb, self.GHb, self.ESKb = nb(8), nb(FC), Buf()
            self.wmatb = {n: Buf() for n in WORDER}
            constb = Buf()
            PE, ACT, DVE, POOL, SP = self.PE, self.ACT, self.DVE, self.POOL, self.SP

            self.dma(SP, self.PP[:, :], pp_d, self.isem, writes=[constb])
            self.dma(POOL, self.IDB[:, :], cmat_d[0], self.psem, writes=[constb])
            self.dma(POOL, self.PMB[:, :], cmat_d[1], self.psem, writes=[constb])
            for q in range(4):
                self.dma(POOL, self.SELB[:, q, :], cmat_d[2 + q], self.psem, writes=[constb])
            self.do(DVE, lambda: nc.vector.memset(self.VD[:, :, :], 0.0), writes=self.VDb)
            self.do(DVE, lambda: nc.vector.memset(self.VD[:].rearrange("p b (g e) -> p (b g) e", e=65)[:, :, 64:65], 1.0), writes=self.VDb)
            self.do(DVE, lambda: nc.vector.memset(self.ONESB[:, :], 1.0), writes=[constb])
            self.do(DVE, lambda: nc.vector.memset(self.INVB[:, :], 1.0 / 128), writes=[constb])
            self.do(ACT, lambda: nc.scalar.activation(out=self.ESK[:, :], in_=self.pp("sinks", 0, 16), func=AF.Exp, bias=self.pp("zero"), scale=1.0), reads=[constb], writes=[self.ESKb])
            self.do(DVE, lambda: nc.vector.tensor_copy(out=self.ESKB[:, :, :].rearrange("p g (a q) -> p (g a) q", a=4), in_=self.ESK[:, :].unsqueeze(2).to_broadcast([128, 16, 64])),
                    reads=[self.ESKb], writes=[self.ESKb])
            for gi in range(3):
                self.dma(SP, self.X[:, 3, :], gpost_d[gi:gi + 1, :].partition_broadcast(128), self.isem, writes=[self.Xb[3]])
                self.do(ACT, lambda: nc.scalar.activation(out=self.GB[:, gi, :], in_=self.X[:, 3, :], func=AF.Copy), reads=[self.Xb[3]], writes=[self.GBb])
            self.do(DVE, lambda: nc.vector.memset(self.KD[:].rearrange("p a h k -> p (a h k)"), 0.0), writes=[self.KDhb] + self.KDtb)
            self.dma(POOL, self.MK[:, :, :], cmkT_d.rearrange("(c p) m -> p c m", p=128), self.psem, writes=[self.MKb])
            self.dma(POOL, self.MV[:, :, :], cmv_d.rearrange("(b p) f -> p b f", p=128), self.psem, writes=[self.MVb])
            for ab in range(2):
                self.dma(POOL, self.KD[:, ab, :, 0:128], ckd_d[:, ab, :, :], self.psem, writes=[self.KDhb])
            self.dma(POOL, self.VD[:, 0, :], cvd_d, self.psem, writes=[self.VDb[0]])
            self.dma(POOL, self.CUH[:, :, :], smix_d.rearrange("p (c k) -> p c k", k=2), self.psem, writes=self.CUHb)
            self.dma(POOL, self.GH[:, :, :], sffn_d.rearrange("p (c k) -> p c k", k=2), self.psem, writes=self.GHb)
            self.dma(POOL, o_kso, csk_d[64:128, :], self.osem)
            self.dma(POOL, o_vso, csv_d[64:128, :], self.osem)
            for n in WORDER:
                K = WSHAPES[n][0]
                step = K // 4
                for i in range(4):
                    self.dma(POOL, self.wbf[n][i * step:(i + 1) * step, :], wsrc[n][i * step:(i + 1) * step, :], cvsem[n], writes=[self.wmatb[n]], nowait=True)
            for e in (PE, DVE, POOL, ACT):
                self._waits(e, [constb], [])
            self.alld = [self.osem, self.isem, self.psem, self.msem, self.csem_t] + self.ysem + self.xsem + self.wsem + list(cvsem.values())
            try:
                self.main_prog(o_ys, o_kTs, o_vs, o_mcs, o_fcs, o_kTp, o_vp, o_y, o_mcp, o_fcp)
            except StopBuild:
                pass
            for e in (PE, ACT, DVE):
                if e.n:
                    nc.gpsimd.wait_ge(e.sem, e.n)
            for ds in self.alld:
                if ds.total:
                    nc.gpsimd.wait_ge(ds.h, ds.total)
        return nc

    def main_prog(self, o_ys, o_kTs, o_vs, o_mcs, o_fcs, o_kTp, o_vp, o_y, o_mcp, o_fcp):
        if True:
            nc = self.nc
            PE, ACT, DVE, POOL, SP = self.PE, self.ACT, self.DVE, self.POOL, self.SP
            self.ckpt("const")
            self.E2 = DVE
            self.plan = tile_plan() + memkv_plan() + tile_plan(True) + tile_plan() * self.nt_main
            self.widx = 0
            self.wissued = 0
            self.tile(TS, self.xs_d, o_ys, 0, 0, last_out={"k": o_kTs, "v": o_vs})
            self.dma(POOL, o_mcs, self.CUH[:, :, :].rearrange("p c k -> p (c k)"), self.osem, reads=self.CUHb)
            self.dma(POOL, o_fcs, self.GH[:, :, :].rearrange("p c k -> p (c k)"), self.osem, reads=self.GHb)
            self.memkv()
            self.ckpt("memkv")
            self.tile(HALO, self.xseg_d[0:HALO, :], None, TS, 1, halo=True)
            fl = self.pp("flag")
            self.do(self.E2, lambda: self.E2.h.tensor_scalar(out=self.GH[:, :, :], in0=self.GH[:, :, :], scalar1=fl, scalar2=None, op0=ALU.mult), reads=self.GHb, writes=self.GHb)
            self.do(self.E2, lambda: self.E2.h.tensor_scalar(out=self.CUH[:, :, :], in0=self.CUH[:, :, :], scalar1=fl, scalar2=None, op0=ALU.mult), reads=self.CUHb, writes=self.CUHb)
            self.E2 = POOL
            for t in range(self.nt_main):
                lo = {"k": o_kTp, "v": o_vp} if t == self.nt_main - 1 else None
                self.tile(512, self.xseg_d[HALO + t * 512:HALO + (t + 1) * 512, :], o_y[t * 512:(t + 1) * 512, :], TS + HALO + t * 512, 2 + t, last_out=lo)
            self.dma(POOL, o_mcp, self.CUH[:, :, :].rearrange("p c k -> p (c k)"), self.osem, reads=self.CUHb)
            self.dma(POOL, o_fcp, self.GH[:, :, :].rearrange("p c k -> p (c k)"), self.osem, reads=self.GHb)
            assert self.widx == len(self.plan), (self.widx, len(self.plan))


def _rope_tables(pos):
    inv = (np.float32(500000.0) ** (-np.arange(0, 16, 2, dtype=np.float32) / np.float32(16))).astype(np.float32)
    ang = (pos.astype(np.float32)[:, None] * inv[None, :]).astype(np.float32)
    cos, sin = np.cos(ang).astype(np.float32), np.sin(ang).astype(np.float32)
    n = pos.shape[0]
    ct = np.ones((128, n), np.float32)
    st = np.zeros((128, n), np.float32)
    for p in range(128):
        d = p % 64
        if d < 16:
            ct[p] = cos[:, d % 8]
            st[p] = -sin[:, d % 8] if d < 8 else sin[:, d % 8]
    return ct, st


def _const_mats():
    ident = np.eye(128, dtype=np.float32)
    pm = np.zeros((128, 128), np.float32)
    for m in range(128):
        d = m % 64
        if d < 16:
            pm[(m // 64) * 64 + (d + 8) % 16, m] = 1.0
    sel = np.zeros((4, 128, 128), np.float32)
    for hh in range(2):
        for ab in range(2):
            for m in range(64):
                sel[hh * 2 + ab, hh * 64 + m, ab * 64 + m] = 1.0
    return np.stack([ident, pm, sel[0], sel[1], sel[2], sel[3]]).astype(np.float32)


def _colmajor(v, nchunk):
    return np.ascontiguousarray(v.reshape(nchunk, 128).T)


_CACHE = {}


def kernel(x_prompt, x_sample, cache_mem_k, cache_mem_v, cache_swa_k, cache_swa_v,
           state_mix_conv, state_ffn_conv, mem_prompt,
           g_mix_pre, w_mix_in, conv_mix_w, g_grp_conv, g_grp_attn, attn_sinks,
           w_mix_out, g_mix_post, g_mem, w_xk, w_xv, g_x_pre, w_xq, w_xo, g_x_post,
           g_ffn_pre, w_gate, w_up, conv_ffn_w, conv_ffn_b, w_down, g_ffn_post):
    f = lambda a: np.ascontiguousarray(np.asarray(a, dtype=np.float32))
    x_prompt, x_sample = f(x_prompt), f(x_sample)
    nt_main = int(os.environ.get("KNT", "8"))
    if "nc" not in _CACHE:
        _CACHE["nc"] = Builder(nt_main).build()
    nc = _CACHE["nc"]
    weights = {"mix_in": f(w_mix_in)[0], "mix_out": f(w_mix_out)[0], "xk": f(w_xk)[0], "xv": f(w_xv)[0], "xq": f(w_xq)[0],
               "xo": f(w_xo)[0], "gate": f(w_gate)[0], "up": f(w_up)[0], "down": f(w_down)[0]}
    cmat = _const_mats()
    sinks = f(attn_sinks)[0]
    s16 = np.zeros((128, 16), np.float32)
    for p in range(128):
        for g in range(4):
            for ab in range(2):
                s16[p, g * 2 + ab] = sinks[4 * g + 2 * (p // 64) + ab]
    cw = f(conv_mix_w)[0]
    fw = f(conv_ffn_w)[0]
    pp_base = np.zeros((128, NPP), np.float32)
    pp_base[:, PPO["g_mix_pre"]:PPO["g_mix_pre"] + 16] = _colmajor(f(g_mix_pre)[0], 16)
    pp_base[:, PPO["g_x_pre"]:PPO["g_x_pre"] + 16] = _colmajor(f(g_x_pre)[0], 16)
    pp_base[:, PPO["g_ffn_pre"]:PPO["g_ffn_pre"] + 16] = _colmajor(f(g_ffn_pre)[0], 16)
    pp_base[:, PPO["g_mem"]:PPO["g_mem"] + 16] = _colmajor(f(g_mem)[0], 16)
    pp_base[:, PPO["g_conv"]:PPO["g_conv"] + 8] = _colmajor(f(g_grp_conv)[0], 8)
    pp_base[:, PPO["g_attn"]:PPO["g_attn"] + 8] = _colmajor(f(g_grp_attn)[0], 8)
    pp_base[:, PPO["cw"]:PPO["cw"] + 24] = np.stack([_colmajor(cw[k], 8) for k in range(3)], axis=2).reshape(128, 24)
    pp_base[:, PPO["fw"]:PPO["fw"] + 132] = np.stack([_colmajor(fw[k], FC) for k in range(3)], axis=2).reshape(128, 132)
    pp_base[:, PPO["fb"]:PPO["fb"] + FC] = _colmajor(f(conv_ffn_b)[0], FC)
    pp_base[:, PPO["sinks"]:PPO["sinks"] + 16] = s16
    pp_base[:, PPO["eps"]] = EPS
    gpost = np.stack([f(g_mix_post)[0], f(g_x_post)[0], f(g_ffn_post)[0]])
    in_maps = []
    ncr = int(os.environ.get("KCORES", str(N_CORES)))
    for c in range(ncr):
        b, half = c // 2, c % 2
        if half == 0:
            xseg = np.concatenate([np.zeros((HALO, D), np.float32), x_prompt[b, 0:SEG]], axis=0)
            pos_h = np.zeros(HALO, np.float32)
            pos_m = np.arange(0, SEG, dtype=np.float32)
        else:
            xseg = x_prompt[b, SEG - HALO:2 * SEG]
            pos_h = np.arange(SEG - HALO, SEG, dtype=np.float32)
            pos_m = np.arange(SEG, 2 * SEG, dtype=np.float32)
        pos = np.concatenate([1024 + np.arange(TS, dtype=np.float32), pos_h, pos_m])
        ct, st = _rope_tables(pos)
        pp = pp_base.copy()
        pp[:, PPO["flag"]] = float(half)
        if half == 0:
            mo = PPO["mask"] + 2 * 2
            pp[:, mo] = -30000.0
            pp[64:128, mo + 1] = -30000.0
        ck = f(cache_swa_k)[0, c]
        cv = f(cache_swa_v)[0, c]
        kT = ck.transpose(2, 1, 0)
        ckd = np.zeros((128, 2, 4, 128), np.float32)
        ckd[0:64, 0] = kT
        ckd[64:128, 1] = kT
        cvd = np.ascontiguousarray(np.concatenate([cv, np.ones((128, 4, 1), np.float32)], axis=2).reshape(128, 260))
        m = {
            "xseg": np.ascontiguousarray(xseg), "xs": x_sample[c], "memp": f(mem_prompt)[b],
            "cmkT": np.ascontiguousarray(f(cache_mem_k)[0, c].reshape(256, D).T), "cmv": f(cache_mem_v)[0, c].reshape(256, D),
            "ckd": ckd, "cvd": cvd, "csk": ck.reshape(128, 256), "csv": cv.reshape(128, 256),
            "smix": np.ascontiguousarray(f(state_mix_conv)[0, c].reshape(2, 8, 128).transpose(2, 1, 0).reshape(128, 16)),
            "sffn": np.ascontiguousarray(f(state_ffn_conv)[0, c].reshape(2, FC, 128).transpose(2, 1, 0).reshape(128, 88)),
            "pp": pp, "gpost": gpost, "cmat": cmat, "cs": np.stack([ct, st]),
        }
        for n in WORDER:
            m["w_" + n] = weights[n]
        in_maps.append(m)
    res = run_bass_kernel_spmd(nc, in_maps, core_ids=list(range(ncr)))
    R = res.results
    yp = np.zeros((4, 2 * SEG, D), np.float32)
    ys = np.zeros((8, TS, D), np.float32)
    mk = np.zeros((1, 4, 256, 4, 512), np.float32)
    mv = np.zeros((1, 4, 256, 4, 512), np.float32)
    skp = np.zeros((1, 4, 128, 4, 64), np.float32)
    svp = np.zeros((1, 4, 128, 4, 64), np.float32)
    mcp = np.zeros((1, 4, 2, 1024), np.float32)
    fcp = np.zeros((1, 4, 2, DFF), np.float32)
    sks = np.zeros((1, 8, 128, 4, 64), np.float32)
    svs = np.zeros((1, 8, 128, 4, 64), np.float32)
    mcs = np.zeros((1, 8, 2, 1024), np.float32)
    fcs = np.zeros((1, 8, 2, DFF), np.float32)
    unfm = lambda a, nch: np.asarray(a).reshape(128, nch, 2).transpose(2, 1, 0).reshape(2, nch * 128)
    for c in range(ncr):
        b, half = c // 2, c % 2
        r = R[c]
        yp[b, half * SEG:(half + 1) * SEG] = r["o_y"]
        ys[c] = r["o_ys"]
        if half == 0:
            mk[0, b] = np.asarray(r["o_mkT"]).T.reshape(256, 4, 512)
            mv[0, b] = np.asarray(r["o_mv"]).reshape(256, 4, 512)
        else:
            skp[0, b] = np.asarray(r["o_kTp"]).T.reshape(128, 4, 64)
            svp[0, b] = np.asarray(r["o_vp"]).reshape(128, 4, 64)
            mcp[0, b] = unfm(r["o_mcp"], 8)
            fcp[0, b] = unfm(r["o_fcp"], FC)
        sks[0, c, 0:64] = np.asarray(r["o_kso"]).reshape(64, 4, 64)
        sks[0, c, 64:128] = np.asarray(r["o_kTs"]).T.reshape(64, 4, 64)
        svs[0, c, 0:64] = np.asarray(r["o_vso"]).reshape(64, 4, 64)
        svs[0, c, 64:128] = np.asarray(r["o_vs"]).reshape(64, 4, 64)
        mcs[0, c] = unfm(r["o_mcs"], 8)
        fcs[0, c] = unfm(r["o_fcs"], FC)
    return (yp, ys, mk, mv, skp, svp, mcp, fcp, sks, svs, mcs, fcs)
```

```python
import os
import numpy as np
from contextlib import ExitStack
import concourse.bass as bass
import concourse.mybir as mybir
from concourse.bass_utils import run_bass_kernel_spmd

F32 = mybir.dt.float32
BF16 = mybir.dt.bfloat16
AF = mybir.ActivationFunctionType
ALU = mybir.AluOpType
AX = mybir.AxisListType

D = 2048
NIN = 4608
DFF = 5632
FC = 44
EPS = 1e-6
NSLOT = 3
SEG = 4096
HALO = 256
TS = 64
TOT = TS + HALO + SEG
N_CORES = 8

WSHAPES = {
    "mix_in": (D, NIN), "mix_out": (D, D), "xk": (D, D), "xv": (D, D), "xq": (D, D),
    "xo": (D, D), "gate": (D, DFF), "up": (D, DFF), "down": (DFF, D),
}
WORDER = ["mix_in", "mix_out", "xq", "xo", "gate", "up", "down", "xk", "xv"]

PPO = {}
_o = 0
for _n, _w in [("g_mix_pre", 16), ("g_x_pre", 16), ("g_ffn_pre", 16), ("g_mem", 16), ("g_conv", 8),
               ("g_attn", 8), ("cw", 24), ("fw", 132), ("fb", 44), ("sinks", 16), ("flag", 1),
               ("eps", 1), ("zero", 1), ("mask", 20)]:
    PPO[_n] = _o
    _o += _w
NPP = _o


class Buf:
    __slots__ = ("w", "r")

    def __init__(self):
        self.w = None
        self.r = {}


class DSem:
    def __init__(self, h):
        self.h = h
        self.total = 0


class Eng:
    def __init__(self, name, h, sem):
        self.name, self.h, self.sem = name, h, sem
        self.n = 0
        self.waited = {}


def tile_plan(halo=False):
    u = []
    for r in range(4):
        for part in (1, 2, 0):
            u.append(("mix_in", 0, 16, part * 1024 + r * 256, 256))
    for r in range(4):
        u.append(("mix_in", 0, 16, 3072 + r * 256, 256))
    u.append(("mix_in", 0, 16, 4096, 256))
    u.append(("mix_in", 0, 16, 4352, 256))
    for ob in range(4):
        for un in range(2):
            u.append(("mix_out", un * 8, 8, ob * 512, 512))
    for r in range(8):
        u.append(("xq", 0, 16, r * 256, 256))
    for ob in range(4):
        for un in range(2):
            u.append(("xo", un * 8, 8, ob * 512, 512))
    for p in range(22):
        u.append(("gate", 0, 16, p * 256, 256))
        if not halo:
            u.append(("up", 0, 16, p * 256, 256))
    if not halo:
        for ob in range(4):
            for un in range(6):
                u.append(("down", un * 8, min(8, FC - un * 8), ob * 512, 512))
    return u


def memkv_plan():
    u = []
    for r in range(8):
        u.append(("xk", 0, 16, r * 256, 256))
    for ob in range(4):
        for un in range(2):
            u.append(("xv", un * 8, 8, ob * 512, 512))
    return u


class StopBuild(Exception):
    pass


class Builder:
    def __init__(self, nt_main=8):
        self.nt_main = nt_main
        self.stop = os.environ.get("KSTOP", "")
        self.ntile = 0
        self.nc = bass.Bass("TRN2", target_bir_lowering=False)

    def _waits(self, eng, reads, writes):
        need = {}

        def add(tok):
            if tok is None:
                return
            s, v = tok
            if isinstance(s, DSem):
                v = s.total
                key, h = id(s), s.h
            else:
                key, h = id(s), s
            if key not in need or need[key][1] < v:
                need[key] = (h, v)

        for b in reads:
            add(b.w)
        for b in writes:
            add(b.w)
            for t in b.r.values():
                add(t)
        for key, (h, v) in need.items():
            if eng.name == "pe" and h is eng.sem:
                continue
            if eng.waited.get(key, 0) < v:
                eng.h.wait_ge(h, v)
                eng.waited[key] = v

    def do(self, eng, fn, reads=(), writes=(), reg_only=()):
        self._waits(eng, reads, writes)
        ins = fn()
        eng.n += 1
        ins.then_inc(eng.sem, 1)
        tok = (eng.sem, eng.n)
        for b in writes:
            b.w = tok
            b.r = {}
        for b in reads:
            b.r[id(eng.sem)] = tok
        for b in reg_only:
            b.r[id(eng.sem)] = tok

    def dma(self, q, out, in_, dsem, reads=(), writes=(), nowait=False):
        if not nowait:
            self._waits(q, reads, writes)
        q.h.dma_start(out=out, in_=in_).then_inc(dsem.h, 16)
        dsem.total += 16
        tok = (dsem, dsem.total)
        for b in writes:
            b.w = tok
            b.r = {}
        for b in reads:
            b.r[id(dsem)] = tok

    def ckpt(self, label):
        if self.stop and self.stop == f"{self.ntile}:{label}":
            raise StopBuild()

    def ring(self):
        i = self.bank_i % 8
        self.bank_i += 1
        return self.banks[i], self.bankb[i]

    def wview(self, s, nk, ncols):
        return self.WS[:, s, 0:nk * ncols].rearrange("p (k n) -> p k n", k=nk)

    def w_issue(self, i):
        name, k0, nk, c0, ncols = self.plan[i]
        s = i % NSLOT
        src = self.wbf[name][k0 * 128:(k0 + nk) * 128, c0:c0 + ncols].rearrange("(k p) n -> p k n", p=128)
        self.dma(self.SP, self.wview(s, nk, ncols), src, self.wsem[s], reads=[self.wmatb[name]], writes=[self.Wb[s]])

    def w_next(self, spec):
        assert self.plan[self.widx] == spec, (self.widx, self.plan[self.widx], spec)
        while self.wissued < min(len(self.plan), self.widx + NSLOT):
            self.w_issue(self.wissued)
            self.wissued += 1
        s = self.widx % NSLOT
        self.widx += 1
        return self.wview(s, spec[2], spec[4]), self.Wb[s]

    def pp(self, name, i=0, n=1, rows=slice(0, 128)):
        o = PPO[name] + i
        return self.PP[rows, o:o + n]

    def proj_fm(self, W, Wb, j, T, nk=16, split=False):
        bank, bb = self.ring()
        nc = self.nc
        A = self.A
        if split and T == 512:
            for tb in range(4):
                def f():
                    ins = None
                    for kc in range(nk):
                        ins = nc.tensor.matmul(bank[:, tb * 128:(tb + 1) * 128], lhsT=W[:, kc, j * 128:(j + 1) * 128], rhs=A[:, kc, tb * 128:(tb + 1) * 128],
                                               start=(kc == 0), stop=(kc == nk - 1))
                    return ins

                self.do(self.PE, f, reads=[Wb, self.Atb[tb]], writes=[bb], reg_only=self.Ab[0:nk])
            return bank, bb

        def f():
            ins = None
            for kc in range(nk):
                ins = nc.tensor.matmul(bank[:, 0:T], lhsT=W[:, kc, j * 128:(j + 1) * 128], rhs=A[:, kc, 0:T],
                                       start=(kc == 0), stop=(kc == nk - 1))
            return ins

        self.do(self.PE, f, reads=[Wb] + self.Ab[0:nk], writes=[bb])
        return bank, bb

    def prenorm(self, T, TB, nblk, gname):
        nc = self.nc
        X, XN, SM, A = self.X, self.XN, self.SM, self.A
        ss4, sd4, rs4 = SM[:TB, 0:nblk], SM[:TB, 4:4 + nblk], SM[:TB, 36:36 + nblk]
        ssb, sdb, rsb = self.SMb[0], self.SMb[4], self.SMb[36]
        self.do(self.E2, lambda: self.E2.h.memset(ss4, 0.0), writes=[ssb])
        for tb in range(nblk):
            self.do(self.ACT, lambda: nc.scalar.activation(out=XN[:TB, tb % 2, :], in_=X[:TB, tb, :], func=AF.Square, accum_out=SM[:TB, tb:tb + 1]),
                    reads=[self.Xb[tb], ssb], writes=[self.XNb[tb % 2], ssb])
        self.do(self.ACT, lambda: nc.scalar.activation(out=sd4, in_=ss4, func=AF.Sqrt, bias=self.pp("eps", rows=slice(0, TB)), scale=1.0 / D),
                reads=[ssb], writes=[sdb])
        self.do(self.DVE, lambda: nc.vector.reciprocal(out=rs4, in_=sd4), reads=[sdb], writes=[rsb])
        for tb in range(nblk):
            i = tb % 2
            rs = SM[:TB, 36 + tb:37 + tb]
            if tb % 2 == 0:
                self.do(self.ACT, lambda: nc.scalar.activation(out=XN[:TB, i, :], in_=X[:TB, tb, :], func=AF.Copy, scale=rs),
                        reads=[self.Xb[tb], rsb], writes=[self.XNb[i]])
            else:
                self.do(self.DVE, lambda: nc.vector.tensor_scalar(out=XN[:TB, i, :], in0=X[:TB, tb, :], scalar1=rs, scalar2=None, op0=ALU.mult),
                        reads=[self.Xb[tb], rsb], writes=[self.XNb[i]])
            for q4 in range(4):
                bank, bb = self.ring()
                pbb = bank[:].bitcast(BF16)

                def f():
                    ins = None
                    for a in range(4):
                        kc = q4 * 4 + a
                        ins = nc.tensor.transpose(pbb[:, a * TB:(a + 1) * TB], XN[:TB, i, kc * 128:(kc + 1) * 128], self.IDB[:TB, :TB])
                    return ins

                self.do(self.PE, f, reads=[self.XNb[i]], writes=[bb])
                gap = self.pp(gname, q4 * 4, 4).unsqueeze(2).to_broadcast([128, 4, TB])
                self.do(self.DVE, lambda: nc.vector.tensor_tensor(out=A[:, q4 * 4:(q4 + 1) * 4, tb * TB:(tb + 1) * TB],
                                                                  in0=pbb[:, 0:4 * TB].rearrange("p (a b) -> p a b", a=4), in1=gap, op=ALU.mult),
                        reads=[bb], writes=self.Ab[q4 * 4:(q4 + 1) * 4] + [self.Atb[tb]])

    def group_norm(self, T, base, sqbase, gname, slot):
        nc = self.nc
        AR = self.AR
        bank, bb = self.ring()

        def f():
            ins = None
            for i in range(8):
                ins = nc.tensor.matmul(bank[:, 0:T], lhsT=self.ONESB[:, :], rhs=AR[:, sqbase + i, 0:T], start=(i == 0), stop=(i == 7))
            return ins

        self.do(self.PE, f, reads=self.ARb[sqbase:sqbase + 8], writes=[bb])
        rsb_ap = self.RSB[:, 0, 0:T]
        self.do(self.ACT, lambda: nc.scalar.activation(out=rsb_ap, in_=bank[:, 0:T], func=AF.Sqrt, bias=self.pp("eps"), scale=1.0 / 1024),
                reads=[bb], writes=[self.RSBb[slot]])
        self.do(self.DVE, lambda: nc.vector.reciprocal(out=rsb_ap, in_=rsb_ap), reads=[self.RSBb[slot]], writes=[self.RSBb[slot]])
        for i in range(8):
            self.do(self.DVE, lambda: nc.vector.scalar_tensor_tensor(out=AR[:, base + i, 0:T], in0=AR[:, base + i, 0:T], scalar=self.pp(gname, i),
                                                                     in1=rsb_ap, op0=ALU.mult, op1=ALU.mult),
                    reads=[self.ARb[base + i], self.RSBb[slot]], writes=[self.ARb[base + i]])

    def out_proj(self, name, nkc, src, srcb, ys, ysb, gi, T, TB, nblk):
        nc = self.nc
        SM, X = self.SM, self.X
        nun = (nkc + 7) // 8
        yss = SM[:, 8:24]
        self.do(self.E2, lambda: self.E2.h.memset(yss, 0.0), writes=[self.SMb[8]])
        for ob in range(4):
            banks = [self.ring() for _ in range(nblk)]
            for un in range(nun):
                k0 = un * 8
                nk = min(8, nkc - k0)
                W, Wb = self.w_next((name, k0, nk, ob * 512, 512))
                for tb in range(nblk):
                    bank, bb = banks[tb]

                    def f():
                        ins = None
                        for kk in range(nk):
                            ins = nc.tensor.matmul(bank[:TB, :], lhsT=src[:, k0 + kk, tb * TB:(tb + 1) * TB], rhs=W[:, kk, :],
                                                   start=(k0 + kk == 0), stop=(k0 + kk == nkc - 1))
                        return ins

                    self.do(self.PE, f, reads=[Wb] + srcb[k0:k0 + nk], writes=[bb])
            for tb in range(nblk):
                bank, bb = banks[tb]
                self.do(self.ACT, lambda: nc.scalar.activation(out=ys[:TB, tb, ob * 512:(ob + 1) * 512], in_=bank[:TB, :], func=AF.Copy),
                        reads=[bb], writes=[ysb[4 * tb + ob]])
                col = 8 + tb * 4 + ob
                self.do(self.ACT, lambda: nc.scalar.activation(out=self.XN[:TB, 0, 0:512], in_=bank[:TB, :], func=AF.Square, accum_out=SM[:TB, col:col + 1]),
                        reads=[bb, self.SMb[8]], writes=[self.XNb[0], self.SMb[8]])
                self.do(self.DVE, lambda: nc.vector.tensor_tensor(out=ys[:TB, tb, ob * 512:(ob + 1) * 512], in0=ys[:TB, tb, ob * 512:(ob + 1) * 512],
                                                                  in1=self.GB[:TB, gi, ob * 512:(ob + 1) * 512], op=ALU.mult),
                        reads=[ysb[4 * tb + ob], self.GBb], writes=[ysb[4 * tb + ob]])
        st4, sd4, rs4 = SM[:TB, 24:24 + nblk], SM[:TB, 28:28 + nblk], SM[:TB, 32:32 + nblk]
        self.do(self.DVE, lambda: nc.vector.tensor_reduce(out=st4, in_=SM[:TB, 8:8 + 4 * nblk].rearrange("p (t o) -> p t o", o=4), axis=AX.X, op=ALU.add),
                reads=[self.SMb[8]], writes=[self.SMb[24]])
        self.do(self.ACT, lambda: nc.scalar.activation(out=sd4, in_=st4, func=AF.Sqrt, bias=self.pp("eps", rows=slice(0, TB)), scale=1.0 / D),
                reads=[self.SMb[24]], writes=[self.SMb[28]])
        self.do(self.DVE, lambda: nc.vector.reciprocal(out=rs4, in_=sd4), reads=[self.SMb[28]], writes=[self.SMb[32]])
        for tb in range(nblk):
            yb = ysb[4 * tb:4 * tb + 4]
            self.do(self.DVE, lambda: nc.vector.scalar_tensor_tensor(out=X[:TB, tb, :], in0=ys[:TB, tb, :], scalar=SM[:TB, 32 + tb:33 + tb], in1=X[:TB, tb, :],
                                                                     op0=ALU.mult, op1=ALU.add),
                    reads=yb + [self.SMb[32], self.Xb[tb]], writes=[self.Xb[tb]])

    def rope(self, src_ap, srcb, sw_bank, swb, T, out_ap, outb):
        nc = self.nc
        T1, CS = self.T1, self.CS
        self.do(self.DVE, lambda: nc.vector.tensor_tensor(out=T1[:, 0, 0:T], in0=src_ap, in1=CS[:, 0, 0:T], op=ALU.mult),
                reads=[srcb, self.CSb], writes=[self.T1b[0]])
        self.do(self.DVE, lambda: nc.vector.tensor_tensor(out=T1[:, 1, 0:T], in0=sw_bank[:, 0:T], in1=CS[:, 1, 0:T], op=ALU.mult),
                reads=[swb, self.CSb], writes=[self.T1b[1]])
        self.do(self.DVE, lambda: nc.vector.tensor_tensor(out=out_ap, in0=T1[:, 0, 0:T], in1=T1[:, 1, 0:T], op=ALU.add),
                reads=[self.T1b[0], self.T1b[1]], writes=[outb])

    def tile(self, T, xsrc, ydst, tcol, tile_idx, last_out=None, halo=False):
        nc = self.nc
        PE, ACT, DVE, POOL = self.PE, self.ACT, self.DVE, self.POOL
        TB = min(128, T)
        nblk = T // TB
        nch = T // 64
        X, A, AR, KD, VD = self.X, self.A, self.AR, self.KD, self.VD
        CB, CV = self.CB, self.CV
        for tb in range(nblk):
            self.dma(self.SP, X[:TB, tb, :], xsrc[tb * TB:(tb + 1) * TB, :], self.xsem[tb], writes=[self.Xb[tb]])
        for k in range(2):
            self.dma(self.SP, self.CS[:, k, 0:T], self.cs_d[k, :, tcol:tcol + T], self.csem_t, writes=[self.CSb])
        self.ckpt("load")
        self.prenorm(T, TB, nblk, "g_mix_pre")
        self.ckpt("prenorm")
        for r in range(4):
            W, Wb = self.w_next(("mix_in", 0, 16, 1024 + r * 256, 256))
            for j in range(2):
                i = 2 * r + j
                bank, bb = self.proj_fm(W, Wb, j, T, split=(r == 0))
                self.do(ACT, lambda: nc.scalar.activation(out=CB[:, j, 2:2 + T], in_=bank[:, 0:T], func=AF.Copy), reads=[bb], writes=[self.CBb[j]])
                self.do(self.E2, lambda: self.E2.h.tensor_copy(out=CB[:, j, 0:2], in_=self.CUH[:, i, :]), reads=[self.CUHb[i]], writes=[self.CBhb[j]])
            W, Wb = self.w_next(("mix_in", 0, 16, 2048 + r * 256, 256))
            for j in range(2):
                i = 2 * r + j
                bank, bb = self.proj_fm(W, Wb, j, T)
                self.do(DVE, lambda: nc.vector.tensor_tensor(out=CB[:, j, 2:2 + T], in0=CB[:, j, 2:2 + T], in1=bank[:, 0:T], op=ALU.mult),
                        reads=[bb, self.CBb[j]], writes=[self.CBb[j]])
                self.do(self.E2, lambda: self.E2.h.tensor_copy(out=self.CUH[:, i, :], in_=CB[:, j, T:T + 2]), reads=[self.CBb[j]], writes=[self.CUHb[i]])
                self.do(DVE, lambda: nc.vector.tensor_scalar(out=CV[:, j, 0:T], in0=CB[:, j, 2:2 + T], scalar1=self.pp("cw", i * 3 + 2), scalar2=None, op0=ALU.mult),
                        reads=[self.CBb[j]], writes=[self.CVb[j]])
                for k in (1, 0):
                    self.do(DVE, lambda: nc.vector.scalar_tensor_tensor(out=CV[:, j, 0:T], in0=CB[:, j, k:k + T], scalar=self.pp("cw", i * 3 + k), in1=CV[:, j, 0:T],
                                                                        op0=ALU.mult, op1=ALU.add),
                            reads=[self.CBb[j], self.CBhb[j], self.CVb[j]], writes=[self.CVb[j]])
            W, Wb = self.w_next(("mix_in", 0, 16, 0 + r * 256, 256))
            for j in range(2):
                i = 2 * r + j
                bank, bb = self.proj_fm(W, Wb, j, T)
                self.do(DVE, lambda: nc.vector.tensor_tensor(out=AR[:, i, 0:T], in0=bank[:, 0:T], in1=CV[:, j, 0:T], op=ALU.mult),
                        reads=[bb, self.CVb[j]], writes=[self.ARb[i]])
                self.do(ACT, lambda: nc.scalar.activation(out=AR[:, 8 + i, 0:T], in_=AR[:, i, 0:T], func=AF.Square), reads=[self.ARb[i]], writes=[self.ARb[8 + i]])
        self.ckpt("conv")
        self.group_norm(T, 0, 8, "g_conv", 0)
        self.ckpt("convnorm")
        def q_tail(i, qb, qbb):
            bank2, bb2 = self.ring()
            self.do(PE, lambda: nc.tensor.matmul(bank2[:, 0:T], lhsT=self.PMB[:, :], rhs=qb, start=True, stop=True), reads=[qbb], writes=[bb2])
            self.rope(qb, qbb, bank2, bb2, T, AR[:, 16 + i, 0:T], self.ARb[16 + i])

        pend_q = None
        for r in range(4):
            W, Wb = self.w_next(("mix_in", 0, 16, 3072 + r * 256, 256))
            for j in range(2):
                i = 2 * r + j
                bank, bb = self.proj_fm(W, Wb, j, T)
                qb, qbb = self.QB[:, i % 2, 0:T], self.QBb[i % 2]
                self.do(ACT, lambda: nc.scalar.activation(out=qb, in_=bank[:, 0:T], func=AF.Copy), reads=[bb], writes=[qbb])
                if pend_q is not None:
                    q_tail(*pend_q)
                pend_q = (i, qb, qbb)
        self.ckpt("q")
        Wk, Wkb = self.w_next(("mix_in", 0, 16, 4096, 256))

        def k_proj(j):
            bank, bb = self.proj_fm(Wk, Wkb, j, T)
            self.do(ACT, lambda: nc.scalar.activation(out=self.KF[:, j, 0:T], in_=bank[:, 0:T], func=AF.Copy), reads=[bb], writes=[self.KFb[j]])

        def k_rope(j):
            self.do(DVE, lambda: nc.vector.tensor_copy(out=self.QB[:, j, 0:T], in_=self.KF[:, j, 0:T]), reads=[self.KFb[j]], writes=[self.QBb[j]])
            bank2, bb2 = self.ring()
            self.do(PE, lambda: nc.tensor.matmul(bank2[:, 0:T], lhsT=self.PMB[:, :], rhs=self.QB[:, j, 0:T], start=True, stop=True), reads=[self.QBb[j]], writes=[bb2])
            self.rope(self.KF[:, j, 0:T], self.KFb[j], bank2, bb2, T, self.KR[:, j, 0:T], self.KRb[j])
            self.do(DVE, lambda: nc.vector.tensor_copy(out=self.KRB[:, j, 0:T], in_=self.KR[:, j, 0:T]), reads=[self.KRb[j]], writes=[self.KRBb[j]])

        def k_sel(j):
            for hh in range(2):
                h = 2 * j + hh
                for ab in range(2):
                    bank3, bb3 = self.ring()
                    self.do(PE, lambda: nc.tensor.matmul(bank3[:, 0:T], lhsT=self.SELB[:, hh * 2 + ab, :], rhs=self.KRB[:, j, 0:T], start=True, stop=True), reads=[self.KRBb[j]], writes=[bb3])
                    self.do(ACT, lambda: nc.scalar.activation(out=KD[:, ab, h, 128:128 + T], in_=bank3[:, 0:T], func=AF.Copy), reads=[bb3], writes=[self.KDtb[h]])

        def v_proj(tb):
            bank, bb = self.ring()

            def f():
                ins = None
                for kc in range(16):
                    ins = nc.tensor.matmul(bank[:TB, 0:256], lhsT=A[:, kc, tb * TB:(tb + 1) * TB], rhs=Wv[:, kc, 0:256], start=(kc == 0), stop=(kc == 15))
                return ins

            self.do(PE, f, reads=[Wvb] + self.Ab, writes=[bb])
            self.do(ACT, lambda: nc.scalar.activation(out=self.VF[:TB, 0, :], in_=bank[:TB, 0:256], func=AF.Copy), reads=[bb], writes=[self.VFb[0]])
            vin = self.VF[:TB, 0, :].rearrange("p (g d) -> p g d", g=4)
            vout = VD[:TB, 1 + tb, :].rearrange("p (g e) -> p g e", e=65)[:, :, 0:64]
            self.do(DVE, lambda: nc.vector.tensor_copy(out=vout, in_=vin), reads=[self.VFb[0]], writes=[self.VDb[1 + tb]])
            if last_out is not None and tb == nblk - 1:
                self.dma(POOL, last_out["v"], self.VF[:TB, 0, :], self.osem, reads=[self.VFb[0]])

        k_proj(0)
        q_tail(*pend_q)
        k_proj(1)
        Wv, Wvb = self.w_next(("mix_in", 0, 16, 4352, 256))
        ksteps = [lambda: k_rope(0), lambda: k_rope(1), lambda: k_sel(0), lambda: k_sel(1)]
        vsteps = [(lambda tb=tb: v_proj(tb)) for tb in range(nblk)]
        for idx in range(max(len(ksteps), len(vsteps))):
            if idx < len(vsteps):
                vsteps[idx]()
            if idx < len(ksteps):
                ksteps[idx]()
        if last_out is not None:
            kdst = last_out["k"]
            for j in range(2):
                self.dma(POOL, kdst[j * 128:(j + 1) * 128, :], self.KR[:, j, T - kdst.shape[1]:T], self.osem, reads=[self.KRb[j]])
        self.ckpt("v")
        units = [(j, g) for j in range(nch) for g in range(4)]
        for k in range(4):
            zr = slice(64, 128) if k < 2 else slice(0, 64)
            self.do(self.E2, lambda: self.E2.h.memset(AR[zr, 24 + k, 256:512], 0.0), writes=[self.ARb[24 + k]])

        def geom(j):
            if j % 2 == 0:
                return j // 2, j // 2 + 1, slice(0, 64), ((j - 2, 0), (j - 1, 0), (j, 1))
            return (j + 1) // 2, (j - 1) // 2, slice(64, 128), ((j - 2, 1), (j - 1, 0), (j, 0))

        def emit_S(u):
            j, g = units[u]
            bank, bb = self.ring()
            mm = geom(j)[3]

            def f():
                ins = None
                for m, reg in mm:
                    par = m % 2
                    for ab in range(2):
                        c0 = reg * 256 + ab * 128
                        ins = nc.tensor.matmul(bank[par * 64:(par + 1) * 64, c0:c0 + 128].rearrange("p (a q) -> p a q", a=2),
                                               lhsT=KD[:, ab, g, (m + 2) * 64:(m + 3) * 64],
                                               rhs=AR[:, 16 + 2 * g:16 + 2 * g + 2, j * 64:(j + 1) * 64], start=True, stop=True)
                return ins

            rd = [self.KDtb[g], self.ARb[16 + 2 * g], self.ARb[17 + 2 * g]]
            if j < 2:
                rd.append(self.KDhb)
            self.do(PE, f, reads=rd, writes=[bb])
            return bank, bb

        def post(u, j, g, od, odb):
            k = u % 2
            rd3 = self.RD[:, 2 * k:2 * k + 2].unsqueeze(2)
            den3 = od[:, 0:130].rearrange("p (a e) -> p a e", e=65)[:, :, 64:65]
            esk3 = self.ESK[:, g * 2:(g + 1) * 2].unsqueeze(2)
            self.do(DVE, lambda: nc.vector.tensor_tensor(out=rd3, in0=den3, in1=esk3, op=ALU.add), reads=[odb, self.ESKb], writes=[self.RDb[k]])
            self.do(DVE, lambda: nc.vector.reciprocal(out=self.RD[:, 2 * k:2 * k + 2], in_=self.RD[:, 2 * k:2 * k + 2]), reads=[self.RDb[k]], writes=[self.RDb[k]])
            for ab in range(2):
                self.do(DVE, lambda: nc.vector.tensor_scalar(out=self.ON[:, k, ab * 64:(ab + 1) * 64], in0=od[:, ab * 65:ab * 65 + 64],
                                                             scalar1=self.RD[:, 2 * k + ab:2 * k + ab + 1], scalar2=None, op0=ALU.mult),
                        reads=[odb, self.RDb[k]], writes=[self.ONb[k]])
            tp, tpb_ = self.ring()
            tpv = tp[:].bitcast(BF16)
            self.do(PE, lambda: nc.tensor.transpose(tpv[:, 0:128], self.ON[:, k, :], self.IDB[:, :]), reads=[self.ONb[k]], writes=[tpb_])
            pend2.append((j, g, tpv, tpb_))

        def post2(j, g, tpv, tpb_):
            self.do(DVE, lambda: nc.vector.tensor_copy(out=AR[:, 8 + 2 * g:8 + 2 * g + 2, j * 64:(j + 1) * 64],
                                                       in_=tpv[:, 0:128].rearrange("p (a q) -> p a q", a=2)),
                    reads=[tpb_], writes=[self.ARb[8 + 2 * g], self.ARb[9 + 2 * g]])

        pend = []
        pend2 = []
        cnt = [0, 0]
        Sb = emit_S(0)
        for u, (j, g) in enumerate(units):
            bank, bb = Sb
            pj = j % 2
            pk = 24 + 2 * pj + cnt[pj] % 2
            cnt[pj] += 1
            pt, ptb = AR[:, pk, :], self.ARb[pk]
            blkF, blkH, hrows, _ = geom(j)
            biasF = self.pp("mask", 2 * tile_idx) if j == 0 else self.pp("zero")
            biasH = self.pp("mask", 2 * tile_idx + 1, rows=hrows) if j == 1 else self.pp("zero", rows=hrows)
            self.do(ACT, lambda: nc.scalar.activation(out=pt[:, 0:256], in_=bank[:, 0:256], func=AF.Exp, bias=biasF, scale=0.125), reads=[bb], writes=[ptb])
            self.do(ACT, lambda: nc.scalar.activation(out=pt[hrows, 256:512], in_=bank[hrows, 256:512], func=AF.Exp, bias=biasH, scale=0.125),
                    reads=[bb, ptb], writes=[ptb])
            if u + 1 < len(units):
                Sb = emit_S(u + 1)
            od, odb = self.ring()

            def f():
                ins = None
                for ab in range(2):
                    for idx, (blk, reg) in enumerate(((blkF, 0), (blkH, 1))):
                        c0 = reg * 256 + ab * 128
                        ins = nc.tensor.matmul(od[:, ab * 65:(ab + 1) * 65], lhsT=pt[:, c0:c0 + 128], rhs=VD[:, blk, g * 65:(g + 1) * 65],
                                               start=(idx == 0), stop=(idx == 1))
                return ins

            self.do(PE, f, reads=[ptb, self.VDb[blkF], self.VDb[blkH]], writes=[odb])
            pend.append((u, j, g, od, odb))
            if len(pend2) > 0:
                post2(*pend2.pop(0))
            if len(pend) > 1:
                post(*pend.pop(0))
        while pend:
            post(*pend.pop(0))
        while pend2:
            post2(*pend2.pop(0))
        for i in range(8):
            self.do(ACT, lambda: nc.scalar.activation(out=AR[:, 16 + i, 0:T], in_=AR[:, 8 + i, 0:T], func=AF.Square), reads=[self.ARb[8 + i]], writes=[self.ARb[16 + i]])
        self.group_norm(T, 8, 16, "g_attn", 1)
        self.ckpt("attn")
        if T >= 128:
            self.do(self.E2, lambda: self.E2.h.tensor_copy(out=KD[:, :, :, 0:128].rearrange("p a h k -> p (a h) k"), in_=KD[:, :, :, T:T + 128].rearrange("p a h k -> p (a h) k")), reads=self.KDtb, writes=[self.KDhb])
            self.do(self.E2, lambda: self.E2.h.tensor_copy(out=VD[:, 0, :], in_=VD[:, nblk, :]), reads=[self.VDb[nblk]], writes=[self.VDb[0]])
        ysA = A[:].rearrange("p (t c) f -> p t (c f)", t=4)
        ysB = AR[:, 0:16, :].rearrange("p (t c) f -> p t (c f)", t=4)
        self.out_proj("mix_out", 16, AR, self.ARb, ysA, self.Ab, 0, T, TB, nblk)
        self.ckpt("mixout")
        self.prenorm(T, TB, nblk, "g_x_pre")
        for r in range(8):
            W, Wb = self.w_next(("xq", 0, 16, r * 256, 256))
            for j in range(2):
                c = 2 * r + j
                bank, bb = self.proj_fm(W, Wb, j, T, split=(r == 0))
                self.do(ACT, lambda: nc.scalar.activation(out=AR[:, c, 0:T], in_=bank[:, 0:T], func=AF.Copy), reads=[bb], writes=[self.ARb[c]])
        for h in range(4):
            xo = 28 + 2 * (h % 2)
            xpb = [self.ARb[xo], self.ARb[xo + 1]]
            for mb in range(2):
                bank, bb = self.ring()

                def f():
                    ins = None
                    for kk in range(4):
                        ins = nc.tensor.matmul(bank[:, 0:T], lhsT=self.MK[:, 4 * h + kk, mb * 128:(mb + 1) * 128], rhs=AR[:, 4 * h + kk, 0:T], start=(kk == 0), stop=(kk == 3))
                    return ins

                self.do(PE, f, reads=[self.MKb] + self.ARb[4 * h:4 * h + 4], writes=[bb])
                self.do(ACT, lambda: nc.scalar.activation(out=AR[:, xo + mb, 0:T], in_=bank[:, 0:T], func=AF.Exp, bias=self.pp("zero"), scale=float(512 ** -0.5)), reads=[bb], writes=[xpb[mb]])
            bank, bb = self.ring()

            def f():
                ins = None
                for mb in range(2):
                    ins = nc.tensor.matmul(bank[:, 0:T], lhsT=self.ONESB[:, :], rhs=AR[:, xo + mb, 0:T], start=(mb == 0), stop=(mb == 1))
                return ins

            self.do(PE, f, reads=xpb, writes=[bb])
            rd, rdb = self.T1[:, h % 2, 0:T], self.T1b[h % 2]
            self.do(DVE, lambda: nc.vector.reciprocal(out=rd, in_=bank[:, 0:T]), reads=[bb], writes=[rdb])
            for dc in range(4):
                bank, bb = self.ring()
                fc = 4 * h + dc

                def f():
                    ins = None
                    for mb in range(2):
                        ins = nc.tensor.matmul(bank[:, 0:T], lhsT=self.MV[:, mb, fc * 128:(fc + 1) * 128], rhs=AR[:, xo + mb, 0:T], start=(mb == 0), stop=(mb == 1))
                    return ins

                self.do(PE, f, reads=xpb + [self.MVb], writes=[bb])
                self.do(DVE, lambda: nc.vector.tensor_tensor(out=A[:, fc, 0:T], in0=bank[:, 0:T], in1=rd, op=ALU.mult), reads=[bb, rdb], writes=[self.Ab[fc]])
        self.ckpt("xattn")
        self.out_proj("xo", 16, A, self.Ab, ysB, self.ARb, 1, T, TB, nblk)
        self.ckpt("xo")
        self.prenorm(T, TB, nblk, "g_ffn_pre")
        for p in range(22):
            W, Wb = self.w_next(("gate", 0, 16, p * 256, 256))
            for j in range(2):
                c = 2 * p + j
                bank, bb = self.proj_fm(W, Wb, j, T, split=(p == 0))
                self.do(ACT, lambda: nc.scalar.activation(out=CB[:, j, 2:2 + T], in_=bank[:, 0:T], func=AF.Copy), reads=[bb], writes=[self.CBb[j]])
                self.do(self.E2, lambda: self.E2.h.tensor_copy(out=CB[:, j, 0:2], in_=self.GH[:, c, :]), reads=[self.GHb[c]], writes=[self.CBhb[j]])
                self.do(self.E2, lambda: self.E2.h.tensor_copy(out=self.GH[:, c, :], in_=CB[:, j, T:T + 2]), reads=[self.CBb[j]], writes=[self.GHb[c]])
                if halo:
                    continue
                self.do(DVE, lambda: nc.vector.tensor_scalar(out=CV[:, j, 0:T], in0=CB[:, j, 2:2 + T], scalar1=self.pp("fw", c * 3 + 2), scalar2=self.pp("fb", c), op0=ALU.mult, op1=ALU.add),
                        reads=[self.CBb[j]], writes=[self.CVb[j]])
                for k in (1, 0):
                    self.do(DVE, lambda: nc.vector.scalar_tensor_tensor(out=CV[:, j, 0:T], in0=CB[:, j, k:k + T], scalar=self.pp("fw", c * 3 + k), in1=CV[:, j, 0:T],
                                                                        op0=ALU.mult, op1=ALU.add),
                            reads=[self.CBb[j], self.CBhb[j], self.CVb[j]], writes=[self.CVb[j]])
                if halo:
                    continue
                self.do(ACT, lambda: nc.scalar.activation(out=CV[:, j, 0:T], in_=CV[:, j, 0:T], func=AF.Silu), reads=[self.CVb[j]], writes=[self.CVb[j]])
            if halo:
                continue
            W, Wb = self.w_next(("up", 0, 16, p * 256, 256))
            for j in range(2):
                c = 2 * p + j
                bank, bb = self.proj_fm(W, Wb, j, T)
                self.do(DVE, lambda: nc.vector.tensor_tensor(out=AR[:, c, 0:T], in0=CV[:, j, 0:T], in1=bank[:, 0:T], op=ALU.mult), reads=[self.CVb[j], bb], writes=[self.ARb[c]])
        self.ckpt("ffn")
        if halo:
            self.ntile += 1
            return
        self.out_proj("down", FC, AR, self.ARb, ysA, self.Ab, 2, T, TB, nblk)
        self.ckpt("down")
        if ydst is not None:
            for tb in range(nblk):
                self.dma(POOL, ydst[tb * TB:(tb + 1) * TB, :], X[:TB, tb, :], self.ysem[tb], reads=[self.Xb[tb]])
        self.ckpt("tile")
        self.ntile += 1

    def memkv(self):
        nc = self.nc
        PE, ACT, DVE, POOL = self.PE, self.ACT, self.DVE, self.POOL
        X, A = self.X, self.A
        for tb in range(2):
            self.dma(self.SP, X[:, tb, :], self.memp_d[tb * 128:(tb + 1) * 128, :], self.xsem[tb], writes=[self.Xb[tb]])
        self.prenorm(256, 128, 2, "g_mem")
        for r in range(8):
            W, Wb = self.w_next(("xk", 0, 16, r * 256, 256))
            for j in range(2):
                c = 2 * r + j
                bank, bb = self.proj_fm(W, Wb, j, 256)
                st, stb = self.T1[:, c % 2, 0:256], self.T1b[c % 2]
                self.do(ACT, lambda: nc.scalar.activation(out=st, in_=bank[:, 0:256], func=AF.Copy), reads=[bb], writes=[stb])
                self.do(DVE, lambda: nc.vector.tensor_copy(out=self.MK[:, c, :], in_=st), reads=[stb], writes=[self.MKb])
                self.dma(ACT, self.o_mkT[c * 128:(c + 1) * 128, :], st, self.msem, reads=[stb])
        for ob in range(4):
            banks = [self.ring() for _ in range(2)]
            for un in range(2):
                W, Wb = self.w_next(("xv", un * 8, 8, ob * 512, 512))
                for mb in range(2):
                    bank, bb = banks[mb]

                    def f():
                        ins = None
                        for kk in range(8):
                            ins = nc.tensor.matmul(bank[:, :], lhsT=A[:, un * 8 + kk, mb * 128:(mb + 1) * 128], rhs=W[:, kk, :], start=(un == 0 and kk == 0), stop=(un == 1 and kk == 7))
                        return ins

                    self.do(PE, f, reads=[Wb] + self.Ab[un * 8:un * 8 + 8], writes=[bb])
            for mb in range(2):
                bank, bb = banks[mb]
                st, stb = self.T1[:, mb, :], self.T1b[mb]
                self.do(ACT, lambda: nc.scalar.activation(out=st, in_=bank[:, :], func=AF.Copy), reads=[bb], writes=[stb])
                self.do(DVE, lambda: nc.vector.tensor_copy(out=self.MV[:, mb, ob * 512:(ob + 1) * 512], in_=st), reads=[stb], writes=[self.MVb])
                self.dma(ACT, self.o_mv[mb * 128:(mb + 1) * 128, ob * 512:(ob + 1) * 512], st, self.msem, reads=[stb])

    def build(self):
        nc = self.nc
        dt = nc.dram_tensor
        self.xseg_d = dt("xseg", [HALO + SEG, D], F32, kind="ExternalInput").ap()
        self.xs_d = dt("xs", [TS, D], F32, kind="ExternalInput").ap()
        self.memp_d = dt("memp", [256, D], F32, kind="ExternalInput").ap()
        cmkT_d = dt("cmkT", [D, 256], F32, kind="ExternalInput").ap()
        cmv_d = dt("cmv", [256, D], F32, kind="ExternalInput").ap()
        ckd_d = dt("ckd", [128, 2, 4, 128], F32, kind="ExternalInput").ap()
        cvd_d = dt("cvd", [128, 260], F32, kind="ExternalInput").ap()
        csk_d = dt("csk", [128, 256], F32, kind="ExternalInput").ap()
        csv_d = dt("csv", [128, 256], F32, kind="ExternalInput").ap()
        smix_d = dt("smix", [128, 16], F32, kind="ExternalInput").ap()
        sffn_d = dt("sffn", [128, 88], F32, kind="ExternalInput").ap()
        pp_d = dt("pp", [128, NPP], F32, kind="ExternalInput").ap()
        gpost_d = dt("gpost", [3, D], F32, kind="ExternalInput").ap()
        cmat_d = dt("cmat", [6, 128, 128], F32, kind="ExternalInput").ap()
        self.cs_d = dt("cs", [2, 128, TOT], F32, kind="ExternalInput").ap()
        wsrc = {n: dt("w_" + n, list(WSHAPES[n]), F32, kind="ExternalInput").ap() for n in WORDER}
        self.wbf = {n: dt("wb_" + n, list(WSHAPES[n]), BF16, kind="Internal").ap() for n in WORDER}
        o_y = dt("o_y", [SEG, D], F32, kind="ExternalOutput").ap()
        o_ys = dt("o_ys", [TS, D], F32, kind="ExternalOutput").ap()
        self.o_mkT = dt("o_mkT", [D, 256], F32, kind="ExternalOutput").ap()
        self.o_mv = dt("o_mv", [256, D], F32, kind="ExternalOutput").ap()
        o_kTp = dt("o_kTp", [256, 128], F32, kind="ExternalOutput").ap()
        o_vp = dt("o_vp", [128, 256], F32, kind="ExternalOutput").ap()
        o_mcp = dt("o_mcp", [128, 16], F32, kind="ExternalOutput").ap()
        o_fcp = dt("o_fcp", [128, 88], F32, kind="ExternalOutput").ap()
        o_kTs = dt("o_kTs", [256, 64], F32, kind="ExternalOutput").ap()
        o_vs = dt("o_vs", [64, 256], F32, kind="ExternalOutput").ap()
        o_kso = dt("o_kso", [64, 256], F32, kind="ExternalOutput").ap()
        o_vso = dt("o_vso", [64, 256], F32, kind="ExternalOutput").ap()
        o_mcs = dt("o_mcs", [128, 16], F32, kind="ExternalOutput").ap()
        o_fcs = dt("o_fcs", [128, 88], F32, kind="ExternalOutput").ap()

        with ExitStack() as es:
            ec = es.enter_context
            sb = lambda n, s, d: ec(nc.sbuf_tensor(n, s, d))
            self.X = sb("X", [128, 4, D], F32)
            self.A = sb("A", [128, 16, 512], BF16)
            self.AR = sb("AR", [128, FC, 512], BF16)
            self.KD = sb("KD", [128, 2, 4, 640], BF16)
            self.VD = sb("VD", [128, 5, 260], BF16)
            self.ON = sb("ON", [128, 2, 128], BF16)
            self.RD = sb("RD", [128, 4], F32)
            self.MK = sb("MK", [128, 16, 256], BF16)
            self.MV = sb("MV", [128, 2, D], BF16)
            self.WS = sb("WS", [128, NSLOT, 4096], BF16)
            self.GB = sb("GB", [128, 3, D], BF16)
            self.CB = sb("CB", [128, 2, 514], F32)
            self.CV = sb("CV", [128, 2, 512], F32)
            self.CS = sb("CS", [128, 2, 512], F32)
            self.T1 = sb("T1", [128, 2, 512], F32)
            self.KF = sb("KF", [128, 2, 512], F32)
            self.KR = sb("KR", [128, 2, 512], F32)
            self.ESKB = sb("ESKB", [128, 4, 256], BF16)
            self.INVB = sb("INVB", [128, 128], BF16)
            self.OSB = sb("OSB", [128, 2, 256], BF16)
            self.KRB = sb("KRB", [128, 2, 512], BF16)
            self.QB = sb("QB", [128, 2, 512], BF16)
            self.XN = sb("XN", [128, 2, D], BF16)
            self.VF = sb("VF", [128, 1, 256], F32)
            self.RSB = sb("RSB", [128, 1, 512], F32)
            self.DE = sb("DE", [128, 2, 256], F32)
            self.SM = sb("SM", [128, 40], F32)
            self.CUH = sb("CUH", [128, 8, 2], F32)
            self.GH = sb("GH", [128, FC, 2], F32)
            self.ESK = sb("ESK", [128, 16], F32)
            self.PP = sb("PP", [128, NPP], F32)
            self.IDB = sb("IDB", [128, 128], BF16)
            self.PMB = sb("PMB", [128, 128], BF16)
            self.SELB = sb("SELB", [128, 4, 128], BF16)
            self.ONESB = sb("ONESB", [128, 128], BF16)
            self.banks = [ec(nc.psum_tensor(f"pb{i}", [128, 512], F32)) for i in range(8)]
            self.bankb = [Buf() for _ in range(8)]
            self.bank_i = 0
            sem = lambda n: ec(nc.semaphore(n))
            self.PE = Eng("pe", nc.tensor, sem("s_pe"))
            self.ACT = Eng("act", nc.scalar, sem("s_act"))
            self.DVE = Eng("dve", nc.vector, sem("s_dve"))
            self.POOL = Eng("pool", nc.gpsimd, sem("s_pool"))
            self.SP = Eng("sp", nc.sync, sem("s_sp"))
            self.wsem = [DSem(sem(f"wsem{i}")) for i in range(NSLOT)]
            self.xsem = [DSem(sem(f"xsem{i}")) for i in range(4)]
            self.ysem = [DSem(sem(f"ysem{i}")) for i in range(4)]
            self.csem_t = DSem(sem("csem_t"))
            self.osem = DSem(sem("osem"))
            self.isem = DSem(sem("isem"))
            self.msem = DSem(sem("msem"))
            self.psem = DSem(sem("psem"))
            cvsem = {n: DSem(sem("cv_" + n)) for n in WORDER}
            nb = lambda n: [Buf() for _ in range(n)]
            self.Xb, self.Ab, self.ARb = nb(4), nb(16), nb(FC)
            self.Atb = nb(4)
            self.KDtb, self.KDhb, self.VDb = nb(4), Buf(), nb(5)
            self.MKb, self.MVb, self.Wb, self.GBb = Buf(), Buf(), nb(NSLOT), Buf()
            self.CBb, self.CBhb, self.CVb, self.CSb, self.T1b = nb(2), nb(2), nb(2), Buf(), nb(2)
            self.KFb, self.KRb, self.KRBb, self.QBb, self.OSBb = nb(2), nb(2), nb(2), nb(2), nb(2)
            self.ONb, self.RDb = nb(2), nb(2)
            self.XNb, self.VFb, self.DEb = nb(2), nb(2), nb(2)
            _r = Buf()
            self.RSBb = [_r, _r]
            self.SMb = nb(40)
            for i in range(9, 24):
                self.SMb[i] = self.SMb[8]
            for base in (0, 4, 24, 28, 32, 36):
                for i in range(base + 1, base + 4):
                    self.SMb[i] = self.SMb[base]
            self.CUHb, self.GHb, self.ESKb = nb(8), nb(FC), Buf()
            self.wmatb = {n: Buf() for n in WORDER}
            constb = Buf()
            PE, ACT, DVE, POOL, SP = self.PE, self.ACT, self.DVE, self.POOL, self.SP

            self.dma(SP, self.PP[:, :], pp_d, self.isem, writes=[constb])
            self.dma(POOL, self.IDB[:, :], cmat_d[0], self.psem, writes=[constb])
            self.dma(POOL, self.PMB[:, :], cmat_d[1], self.psem, writes=[constb])
            for q in range(4):
                self.dma(POOL, self.SELB[:, q, :], cmat_d[2 + q], self.psem, writes=[constb])
            self.do(DVE, lambda: nc.vector.memset(self.VD[:, :, :], 0.0), writes=self.VDb)
            self.do(DVE, lambda: nc.vector.memset(self.VD[:].rearrange("p b (g e) -> p (b g) e", e=65)[:, :, 64:65], 1.0), writes=self.VDb)
            self.do(DVE, lambda: nc.vector.memset(self.ONESB[:, :], 1.0), writes=[constb])
            self.do(DVE, lambda: nc.vector.memset(self.INVB[:, :], 1.0 / 128), writes=[constb])
            self.do(ACT, lambda: nc.scalar.activation(out=self.ESK[:, :], in_=self.pp("sinks", 0, 16), func=AF.Exp, bias=self.pp("zero"), scale=1.0), reads=[constb], writes=[self.ESKb])
            self.do(DVE, lambda: nc.vector.tensor_copy(out=self.ESKB[:, :, :].rearrange("p g (a q) -> p (g a) q", a=4), in_=self.ESK[:, :].unsqueeze(2).to_broadcast([128, 16, 64])),
                    reads=[self.ESKb], writes=[self.ESKb])
            for gi in range(3):
                self.dma(SP, self.X[:, 3, :], gpost_d[gi:gi + 1, :].partition_broadcast(128), self.isem, writes=[self.Xb[3]])
                self.do(ACT, lambda: nc.scalar.activation(out=self.GB[:, gi, :], in_=self.X[:, 3, :], func=AF.Copy), reads=[self.Xb[3]], writes=[self.GBb])
            self.do(DVE, lambda: nc.vector.memset(self.KD[:].rearrange("p a h k -> p (a h k)"), 0.0), writes=[self.KDhb] + self.KDtb)
            self.dma(POOL, self.MK[:, :, :], cmkT_d.rearrange("(c p) m -> p c m", p=128), self.psem, writes=[self.MKb])
            self.dma(POOL, self.MV[:, :, :], cmv_d.rearrange("(b p) f -> p b f", p=128), self.psem, writes=[self.MVb])
            for ab in range(2):
                self.dma(POOL, self.KD[:, ab, :, 0:128], ckd_d[:, ab, :, :], self.psem, writes=[self.KDhb])
            self.dma(POOL, self.VD[:, 0, :], cvd_d, self.psem, writes=[self.VDb[0]])
            self.dma(POOL, self.CUH[:, :, :], smix_d.rearrange("p (c k) -> p c k", k=2), self.psem, writes=self.CUHb)
            self.dma(POOL, self.GH[:, :, :], sffn_d.rearrange("p (c k) -> p c k", k=2), self.psem, writes=self.GHb)
            self.dma(POOL, o_kso, csk_d[64:128, :], self.osem)
            self.dma(POOL, o_vso, csv_d[64:128, :], self.osem)
            for n in WORDER:
                K = WSHAPES[n][0]
                step = K // 4
                for i in range(4):
                    self.dma(POOL, self.wbf[n][i * step:(i + 1) * step, :], wsrc[n][i * step:(i + 1) * step, :], cvsem[n], writes=[self.wmatb[n]], nowait=True)
            for e in (PE, DVE, POOL, ACT):
                self._waits(e, [constb], [])
            self.alld = [self.osem, self.isem, self.psem, self.msem, self.csem_t] + self.ysem + self.xsem + self.wsem + list(cvsem.values())
            try:
                self.main_prog(o_ys, o_kTs, o_vs, o_mcs, o_fcs, o_kTp, o_vp, o_y, o_mcp, o_fcp)
            except StopBuild:
                pass
            for e in (PE, ACT, DVE):
                if e.n:
                    nc.gpsimd.wait_ge(e.sem, e.n)
            for ds in self.alld:
                if ds.total:
                    nc.gpsimd.wait_ge(ds.h, ds.total)
        return nc

    def main_prog(self, o_ys, o_kTs, o_vs, o_mcs, o_fcs, o_kTp, o_vp, o_y, o_mcp, o_fcp):
        if True:
            nc = self.nc
            PE, ACT, DVE, POOL, SP = self.PE, self.ACT, self.DVE, self.POOL, self.SP
            self.ckpt("const")
            self.E2 = DVE
            self.plan = tile_plan() + memkv_plan() + tile_plan(True) + tile_plan() * self.nt_main
            self.widx = 0
            self.wissued = 0
            self.tile(TS, self.xs_d, o_ys, 0, 0, last_out={"k": o_kTs, "v": o_vs})
            self.dma(POOL, o_mcs, self.CUH[:, :, :].rearrange("p c k -> p (c k)"), self.osem, reads=self.CUHb)
            self.dma(POOL, o_fcs, self.GH[:, :, :].rearrange("p c k -> p (c k)"), self.osem, reads=self.GHb)
            self.memkv()
            self.ckpt("memkv")
            self.tile(HALO, self.xseg_d[0:HALO, :], None, TS, 1, halo=True)
            fl = self.pp("flag")
            self.do(self.E2, lambda: self.E2.h.tensor_scalar(out=self.GH[:, :, :], in0=self.GH[:, :, :], scalar1=fl, scalar2=None, op0=ALU.mult), reads=self.GHb, writes=self.GHb)
            self.do(self.E2, lambda: self.E2.h.tensor_scalar(out=self.CUH[:, :, :], in0=self.CUH[:, :, :], scalar1=fl, scalar2=None, op0=ALU.mult), reads=self.CUHb, writes=self.CUHb)
            self.E2 = POOL
            for t in range(self.nt_main):
                lo = {"k": o_kTp, "v": o_vp} if t == self.nt_main - 1 else None
                self.tile(512, self.xseg_d[HALO + t * 512:HALO + (t + 1) * 512, :], o_y[t * 512:(t + 1) * 512, :], TS + HALO + t * 512, 2 + t, last_out=lo)
            self.dma(POOL, o_mcp, self.CUH[:, :, :].rearrange("p c k -> p (c k)"), self.osem, reads=self.CUHb)
            self.dma(POOL, o_fcp, self.GH[:, :, :].rearrange("p c k -> p (c k)"), self.osem, reads=self.GHb)
            assert self.widx == len(self.plan), (self.widx, len(self.plan))


def _rope_tables(pos):
    inv = (np.float32(500000.0) ** (-np.arange(0, 16, 2, dtype=np.float32) / np.float32(16))).astype(np.float32)
    ang = (pos.astype(np.float32)[:, None] * inv[None, :]).astype(np.float32)
    cos, sin = np.cos(ang).astype(np.float32), np.sin(ang).astype(np.float32)
    n = pos.shape[0]
    ct = np.ones((128, n), np.float32)
    st = np.zeros((128, n), np.float32)
    for p in range(128):
        d = p % 64
        if d < 16:
            ct[p] = cos[:, d % 8]
            st[p] = -sin[:, d % 8] if d < 8 else sin[:, d % 8]
    return ct, st


def _const_mats():
    ident = np.eye(128, dtype=np.float32)
    pm = np.zeros((128, 128), np.float32)
    for m in range(128):
        d = m % 64
        if d < 16:
            pm[(m // 64) * 64 + (d + 8) % 16, m] = 1.0
    sel = np.zeros((4, 128, 128), np.float32)
    for hh in range(2):
        for ab in range(2):
            for m in range(64):
                sel[hh * 2 + ab, hh * 64 + m, ab * 64 + m] = 1.0
    return np.stack([ident, pm, sel[0], sel[1], sel[2], sel[3]]).astype(np.float32)


def _colmajor(v, nchunk):
    return np.ascontiguousarray(v.reshape(nchunk, 128).T)


_CACHE = {}


def kernel(x_prompt, x_sample, cache_mem_k, cache_mem_v, cache_swa_k, cache_swa_v,
           state_mix_conv, state_ffn_conv, mem_prompt,
           g_mix_pre, w_mix_in, conv_mix_w, g_grp_conv, g_grp_attn, attn_sinks,
           w_mix_out, g_mix_post, g_mem, w_xk, w_xv, g_x_pre, w_xq, w_xo, g_x_post,
           g_ffn_pre, w_gate, w_up, conv_ffn_w, conv_ffn_b, w_down, g_ffn_post):
    f = lambda a: np.ascontiguousarray(np.asarray(a, dtype=np.float32))
    x_prompt, x_sample = f(x_prompt), f(x_sample)
    nt_main = int(os.environ.get("KNT", "8"))
    if "nc" not in _CACHE:
        _CACHE["nc"] = Builder(nt_main).build()
    nc = _CACHE["nc"]
    weights = {"mix_in": f(w_mix_in)[0], "mix_out": f(w_mix_out)[0], "xk": f(w_xk)[0], "xv": f(w_xv)[0], "xq": f(w_xq)[0],
               "xo": f(w_xo)[0], "gate": f(w_gate)[0], "up": f(w_up)[0], "down": f(w_down)[0]}
    cmat = _const_mats()
    sinks = f(attn_sinks)[0]
    s16 = np.zeros((128, 16), np.float32)
    for p in range(128):
        for g in range(4):
            for ab in range(2):
                s16[p, g * 2 + ab] = sinks[4 * g + 2 * (p // 64) + ab]
    cw = f(conv_mix_w)[0]
    fw = f(conv_ffn_w)[0]
    pp_base = np.zeros((128, NPP), np.float32)
    pp_base[:, PPO["g_mix_pre"]:PPO["g_mix_pre"] + 16] = _colmajor(f(g_mix_pre)[0], 16)
    pp_base[:, PPO["g_x_pre"]:PPO["g_x_pre"] + 16] = _colmajor(f(g_x_pre)[0], 16)
    pp_base[:, PPO["g_ffn_pre"]:PPO["g_ffn_pre"] + 16] = _colmajor(f(g_ffn_pre)[0], 16)
    pp_base[:, PPO["g_mem"]:PPO["g_mem"] + 16] = _colmajor(f(g_mem)[0], 16)
    pp_base[:, PPO["g_conv"]:PPO["g_conv"] + 8] = _colmajor(f(g_grp_conv)[0], 8)
    pp_base[:, PPO["g_attn"]:PPO["g_attn"] + 8] = _colmajor(f(g_grp_attn)[0], 8)
    pp_base[:, PPO["cw"]:PPO["cw"] + 24] = np.stack([_colmajor(cw[k], 8) for k in range(3)], axis=2).reshape(128, 24)
    pp_base[:, PPO["fw"]:PPO["fw"] + 132] = np.stack([_colmajor(fw[k], FC) for k in range(3)], axis=2).reshape(128, 132)
    pp_base[:, PPO["fb"]:PPO["fb"] + FC] = _colmajor(f(conv_ffn_b)[0], FC)
    pp_base[:, PPO["sinks"]:PPO["sinks"] + 16] = s16
    pp_base[:, PPO["eps"]] = EPS
    gpost = np.stack([f(g_mix_post)[0], f(g_x_post)[0], f(g_ffn_post)[0]])
    in_maps = []
    ncr = int(os.environ.get("KCORES", str(N_CORES)))
    for c in range(ncr):
        b, half = c // 2, c % 2
        if half == 0:
            xseg = np.concatenate([np.zeros((HALO, D), np.float32), x_prompt[b, 0:SEG]], axis=0)
            pos_h = np.zeros(HALO, np.float32)
            pos_m = np.arange(0, SEG, dtype=np.float32)
        else:
            xseg = x_prompt[b, SEG - HALO:2 * SEG]
            pos_h = np.arange(SEG - HALO, SEG, dtype=np.float32)
            pos_m = np.arange(SEG, 2 * SEG, dtype=np.float32)
        pos = np.concatenate([1024 + np.arange(TS, dtype=np.float32), pos_h, pos_m])
        ct, st = _rope_tables(pos)
        pp = pp_base.copy()
        pp[:, PPO["flag"]] = float(half)
        if half == 0:
            mo = PPO["mask"] + 2 * 2
            pp[:, mo] = -30000.0
            pp[64:128, mo + 1] = -30000.0
        ck = f(cache_swa_k)[0, c]
        cv = f(cache_swa_v)[0, c]
        kT = ck.transpose(2, 1, 0)
        ckd = np.zeros((128, 2, 4, 128), np.float32)
        ckd[0:64, 0] = kT
        ckd[64:128, 1] = kT
        cvd = np.ascontiguousarray(np.concatenate([cv, np.ones((128, 4, 1), np.float32)], axis=2).reshape(128, 260))
        m = {
            "xseg": np.ascontiguousarray(xseg), "xs": x_sample[c], "memp": f(mem_prompt)[b],
            "cmkT": np.ascontiguousarray(f(cache_mem_k)[0, c].reshape(256, D).T), "cmv": f(cache_mem_v)[0, c].reshape(256, D),
            "ckd": ckd, "cvd": cvd, "csk": ck.reshape(128, 256), "csv": cv.reshape(128, 256),
            "smix": np.ascontiguousarray(f(state_mix_conv)[0, c].reshape(2, 8, 128).transpose(2, 1, 0).reshape(128, 16)),
            "sffn": np.ascontiguousarray(f(state_ffn_conv)[0, c].reshape(2, FC, 128).transpose(2, 1, 0).reshape(128, 88)),
            "pp": pp, "gpost": gpost, "cmat": cmat, "cs": np.stack([ct, st]),
        }
        for n in WORDER:
            m["w_" + n] = weights[n]
        in_maps.append(m)
    res = run_bass_kernel_spmd(nc, in_maps, core_ids=list(range(ncr)))
    R = res.results
    yp = np.zeros((4, 2 * SEG, D), np.float32)
    ys = np.zeros((8, TS, D), np.float32)
    mk = np.zeros((1, 4, 256, 4, 512), np.float32)
    mv = np.zeros((1, 4, 256, 4, 512), np.float32)
    skp = np.zeros((1, 4, 128, 4, 64), np.float32)
    svp = np.zeros((1, 4, 128, 4, 64), np.float32)
    mcp = np.zeros((1, 4, 2, 1024), np.float32)
    fcp = np.zeros((1, 4, 2, DFF), np.float32)
    sks = np.zeros((1, 8, 128, 4, 64), np.float32)
    svs = np.zeros((1, 8, 128, 4, 64), np.float32)
    mcs = np.zeros((1, 8, 2, 1024), np.float32)
    fcs = np.zeros((1, 8, 2, DFF), np.float32)
    unfm = lambda a, nch: np.asarray(a).reshape(128, nch, 2).transpose(2, 1, 0).reshape(2, nch * 128)
    for c in range(ncr):
        b, half = c // 2, c % 2
        r = R[c]
        yp[b, half * SEG:(half + 1) * SEG] = r["o_y"]
        ys[c] = r["o_ys"]
        if half == 0:
            mk[0, b] = np.asarray(r["o_mkT"]).T.reshape(256, 4, 512)
            mv[0, b] = np.asarray(r["o_mv"]).reshape(256, 4, 512)
        else:
            skp[0, b] = np.asarray(r["o_kTp"]).T.reshape(128, 4, 64)
            svp[0, b] = np.asarray(r["o_vp"]).reshape(128, 4, 64)
            mcp[0, b] = unfm(r["o_mcp"], 8)
            fcp[0, b] = unfm(r["o_fcp"], FC)
        sks[0, c, 0:64] = np.asarray(r["o_kso"]).reshape(64, 4, 64)
        sks[0, c, 64:128] = np.asarray(r["o_kTs"]).T.reshape(64, 4, 64)
        svs[0, c, 0:64] = np.asarray(r["o_vso"]).reshape(64, 4, 64)
        svs[0, c, 64:128] = np.asarray(r["o_vs"]).reshape(64, 4, 64)
        mcs[0, c] = unfm(r["o_mcs"], 8)
        fcs[0, c] = unfm(r["o_fcs"], FC)
    return (yp, ys, mk, mv, skp, svp, mcp, fcp, sks, svs, mcs, fcs)
```
